# Optimizing a Trainium2 kernel written in Bass

```python
import math
import jax, jax.numpy as jnp
from jax import lax
import numpy as np


D_MODEL = 1024
BATCH = 8
SEQ = 4096
DEPTH = 2

CHUNK = 64
N_MIXERS = 2
N_POOL_LAYERS = (DEPTH + N_MIXERS - 1) // N_MIXERS
N_SB_LAYERS = DEPTH // N_MIXERS

POOL_WINDOWS = (2, 4, 8, 16)
N_POOL_GROUPS = len(POOL_WINDOWS)
POOL_GROUP_W = D_MODEL // N_POOL_GROUPS

N_HEADS = 16
HEAD_DIM = D_MODEL // N_HEADS
Q_BLOCK = 128

D_FF = ((8 * D_MODEL // 3 + 255) // 256) * 256

DEEPNORM_ALPHA = (2.0 * DEPTH) ** 0.25
DEEPNORM_BETA = (8.0 * DEPTH) ** -0.25
LN_EPS = 1e-5

kernel_name = "hybrid_pool_stickbreak_deepnorm_trunk"


def _layer_norm(x, g, b):
    xf = x.astype(jnp.float32)
    mu = jnp.mean(xf, axis=-1, keepdims=True)
    var = jnp.mean(jnp.square(xf - mu), axis=-1, keepdims=True)
    y = (xf - mu) * lax.rsqrt(var + LN_EPS) * g.astype(jnp.float32) + b.astype(jnp.float32)
    return y.astype(x.dtype)


def _pool_mixer(x, w_grp, scale):
    B, S, D = x.shape
    xg = x.reshape(B, S, N_POOL_GROUPS, POOL_GROUP_W)
    xf = xg.astype(jnp.float32)
    c = jnp.cumsum(xf, axis=1)
    c = jnp.concatenate([jnp.zeros((B, 1, N_POOL_GROUPS, POOL_GROUP_W), jnp.float32), c], axis=1)
    t = jnp.arange(S)
    pooled = []
    for g, w in enumerate(POOL_WINDOWS):
        cg = c[:, :, g]
        hi = cg[:, 1:]
        lo = jnp.pad(cg, ((0, 0), (w - 1, 0), (0, 0)))[:, :S]
        cnt = jnp.minimum(t + 1, w).astype(jnp.float32)[None, :, None]
        pooled.append((hi - lo) / cnt)
    pooled = jnp.stack(pooled, axis=2)
    mix = (pooled - xf).astype(x.dtype)
    y = jnp.einsum('bsgc,gcd->bsgd', mix, w_grp).reshape(B, S, D)
    return y * scale


def _stick_breaking_attention(x, w_qkv, w_o):
    B, S, D = x.shape
    qkv = jnp.einsum('bsd,de->bse', x, w_qkv).reshape(B, S, 3, N_HEADS, HEAD_DIM)
    q = jnp.transpose(qkv[:, :, 0], (0, 2, 1, 3))
    k = jnp.transpose(qkv[:, :, 1], (0, 2, 1, 3))
    v = jnp.transpose(qkv[:, :, 2], (0, 2, 1, 3))
    inv_sqrt_d = 1.0 / math.sqrt(HEAD_DIM)
    outs = []
    for blk in range(S // Q_BLOCK):
        q0 = blk * Q_BLOCK
        q1 = q0 + Q_BLOCK
        qb = q[:, :, q0:q1]
        kb = k[:, :, :q1]
        vb = v[:, :, :q1]
        z = jnp.einsum('bhqd,bhkd->bhqk', qb, kb).astype(jnp.float32) * inv_sqrt_d
        qpos = (q0 + jnp.arange(Q_BLOCK))[:, None]
        kpos = jnp.arange(q1)[None, :]
        mask = kpos < qpos
        log_beta = jax.nn.log_sigmoid(z)
        log_1mb = jnp.where(mask, jax.nn.log_sigmoid(-z), 0.0)
        suffix = lax.cumsum(log_1mb, axis=3, reverse=True) - log_1mb
        a = jnp.where(mask, jnp.exp(log_beta + suffix), 0.0)
        outs.append(jnp.einsum('bhqk,bhkd->bhqd', a.astype(vb.dtype), vb))
    o = jnp.concatenate(outs, axis=2)
    o = jnp.transpose(o, (0, 2, 1, 3)).reshape(B, S, D)
    return jnp.einsum('bsd,de->bse', o, w_o)


def _swiglu(x, w_gate, w_up, w_down):
    h = jax.nn.silu(jnp.einsum('bsd,df->bsf', x, w_gate)) * jnp.einsum('bsd,df->bsf', x, w_up)
    return jnp.einsum('bsf,fd->bsd', h, w_down)


def setup_inputs(seed: int = 0) -> dict:
    key = jax.random.key(seed)
    ks = jax.random.split(key, 16)
    f32 = jnp.float32
    D, F, C = D_MODEL, D_FF, POOL_GROUP_W
    x = jax.random.normal(ks[0], (BATCH, SEQ, D), f32)
    ln_mix_g = 1.0 + 0.02 * jax.random.normal(ks[1], (DEPTH, D), f32)
    ln_mix_b = 0.02 * jax.random.normal(ks[2], (DEPTH, D), f32)
    ln_ffn_g = 1.0 + 0.02 * jax.random.normal(ks[3], (DEPTH, D), f32)
    ln_ffn_b = 0.02 * jax.random.normal(ks[4], (DEPTH, D), f32)
    pool_w = jax.random.normal(ks[5], (N_POOL_LAYERS, N_POOL_GROUPS, C, C), f32) * (C ** -0.5) * DEEPNORM_BETA
    pool_scale = 1.0 + 0.02 * jax.random.normal(ks[6], (N_POOL_LAYERS, D), f32)
    w_qk = jax.random.normal(ks[7], (N_SB_LAYERS, D, 2 * D), f32) * (D ** -0.5)
    w_v = jax.random.normal(ks[8], (N_SB_LAYERS, D, D), f32) * (D ** -0.5) * DEEPNORM_BETA
    w_qkv = jnp.concatenate([w_qk, w_v], axis=-1)
    w_o = jax.random.normal(ks[9], (N_SB_LAYERS, D, D), f32) * (D ** -0.5) * DEEPNORM_BETA
    w_gate = jax.random.normal(ks[10], (DEPTH, D, F), f32) * (D ** -0.5)
    w_up = jax.random.normal(ks[11], (DEPTH, D, F), f32) * (D ** -0.5) * DEEPNORM_BETA
    w_down = jax.random.normal(ks[12], (DEPTH, F, D), f32) * (F ** -0.5) * DEEPNORM_BETA
    return {"x": x, "ln_mix_g": ln_mix_g, "ln_mix_b": ln_mix_b, "ln_ffn_g": ln_ffn_g, "ln_ffn_b": ln_ffn_b,
            "pool_w": pool_w, "pool_scale": pool_scale, "w_qkv": w_qkv, "w_o": w_o,
            "w_gate": w_gate, "w_up": w_up, "w_down": w_down}


def reference(x, ln_mix_g, ln_mix_b, ln_ffn_g, ln_ffn_b, pool_w, pool_scale, w_qkv, w_o, w_gate, w_up, w_down):
    for i in range(DEPTH):
        j = i // N_MIXERS
        if i % N_MIXERS == 0:
            m = _pool_mixer(x, pool_w[j], pool_scale[j])
        else:
            m = _stick_breaking_attention(x, w_qkv[j], w_o[j])
        x = _layer_norm(DEEPNORM_ALPHA * x + m, ln_mix_g[i], ln_mix_b[i])
        f = _swiglu(x, w_gate[i], w_up[i], w_down[i])
        x = _layer_norm(DEEPNORM_ALPHA * x + f, ln_ffn_g[i], ln_ffn_b[i])
    return x
```

```python
import math
import types
from contextlib import ExitStack

import numpy as np
import ml_dtypes
import concourse.bass as bass
import concourse.mybir as mybir
from concourse.bass_utils import run_bass_kernel_spmd

F32 = mybir.dt.float32
BF16 = mybir.dt.bfloat16
AF = mybir.ActivationFunctionType
ALU = mybir.AluOpType

D = 1024
FF = 2816
NFC = FF // 128
S = 4096
T = 512
NTILES = S // T
ALPHA = float((2.0 * 2) ** 0.25)
EPS = 1e-5
NSLOT = 4
WINDOWS = (2, 4, 8, 16)


def _snap(fn):
    if fn is None or fn.__closure__ is None:
        return fn
    cells = []
    for c in fn.__closure__:
        try:
            cells.append(types.CellType(c.cell_contents))
        except ValueError:
            cells.append(c)
    return types.FunctionType(fn.__code__, fn.__globals__, fn.__name__, fn.__defaults__, tuple(cells))


class Sched:
    ENGS = ("pe", "act", "dve", "pool", "sp")

    def __init__(self):
        self.q = {e: [] for e in self.ENGS}
        self.cnt = {}
        self.epoch = 0
        self.lastw = {}
        self.readers = {}
        self.dmacnt = {}
        self.keys = set()
        self.pending = {e: ([], []) for e in self.ENGS}

    def _deps(self, reads, writes):
        waits = set()
        for c in reads:
            if c in self.lastw:
                waits.add(self.lastw[c])
        for c in writes:
            if c in self.lastw:
                waits.add(self.lastw[c])
            for r in self.readers.get(c, ()):
                waits.add(r)
        return waits

    def _commit(self, ev, reads, writes):
        for c in reads:
            self.readers.setdefault(c, []).append(ev)
        for c in writes:
            self.lastw[c] = ev
            self.readers[c] = []

    def op(self, eng, fn, reads=(), writes=(), signal=True):
        fn = _snap(fn)
        waits = self._deps(reads, writes)
        pr, pw = self.pending[eng]
        pr.extend(reads)
        pw.extend(writes)
        if signal:
            key = (eng, self.epoch)
            self.cnt[key] = self.cnt.get(key, 0) + 1
            ev = (key, self.cnt[key])
            self.keys.add(key)
            self.q[eng].append((waits, fn, (key, 1)))
            self._commit(ev, pr, pw)
            self.pending[eng] = ([], [])
            return ev
        self.q[eng].append((waits, fn, None))
        return None

    def dma(self, eng, sem, fn, reads=(), writes=()):
        fn = _snap(fn)
        waits = self._deps(reads, writes)
        key = ("dma", sem)
        self.dmacnt[key] = self.dmacnt.get(key, 0) + 16
        ev = (key, self.dmacnt[key])
        self.keys.add(key)
        self.q[eng].append((waits, fn, (key, 16)))
        self._commit(ev, reads, writes)
        return ev

    def wait_only(self, eng, evs):
        self.q[eng].append((set(evs), None, None))

    def replay(self, nc, sems):
        handles = {"pe": "tensor", "act": "scalar", "dve": "vector", "pool": "gpsimd", "sp": "sync"}
        with nc.Block() as block:
            for eng in self.ENGS:
                def body(e, eng=eng):
                    waited = {}
                    for waits, fn, sig in self.q[eng]:
                        best = {}
                        for k, v in waits:
                            if eng == "pe" and k[0] == "pe":
                                continue
                            if best.get(k, 0) < v:
                                best[k] = v
                        for k, v in sorted(best.items(), key=lambda kv: str(kv[0])):
                            if waited.get(k, 0) < v:
                                e.wait_ge(sems[k], v)
                                waited[k] = v
                        if fn is None:
                            continue
                        ins = fn(e)
                        if sig is not None:
                            ins.then_inc(sems[sig[0]], sig[1])
                getattr(block, handles[eng])(body)


def build(NT=NTILES, LIM=99, PREP=True, ALIM=99, FILL=4, DUMMY=0, FR=140.0):
    nc = bass.Bass("TRN2", target_bir_lowering=False)
    sc = Sched()

    def din(name, shape, dt=F32):
        return nc.dram_tensor(name, list(shape), dt, kind="ExternalInput").ap()

    def dscr(name, shape, dt=BF16):
        return nc.dram_tensor(name, list(shape), dt, kind="Internal").ap()

    x = din("x", [S, D])
    gb = din("gb", [8 * D])
    pool_w = din("pool_w", [4, 256, 256])
    pool_scale = din("pool_scale", [D])
    w_qkv = din("w_qkv", [D, 3 * D])
    w_o = din("w_o", [D, D])
    w_gate = din("w_gate", [2, D, FF])
    w_up = din("w_up", [2, D, FF])
    w_down = din("w_down", [2, FF, D])
    c_ident = din("c_ident", [128, 128])
    c_tri = din("c_tri", [128, 4, 128], BF16)
    c_mask = din("c_mask", [128, 4, 512], BF16)
    c_band = din("c_band", [128, 12, 128], BF16)
    gbc_in = din("gbc", [128, 8, 8])
    out = nc.dram_tensor("out", [S, D], F32, kind="ExternalOutput").ap()

    wgu_s = [dscr(f"wgu_s{l}", [NFC, 128, 2, 8, 128]) for l in range(2)]
    wd_s = [dscr(f"wd_s{l}", [FF, D]) for l in range(2)]
    wqk_s = dscr("wqk_s", [16, 128, 8, 128])
    wv_s = dscr("wv_s", [D, D])
    wo_s = dscr("wo_s", [D, D])
    kt_s = dscr("kt_s", [8, 128, S])
    v_s = dscr("v_s", [S, D])

    es = ExitStack()
    with es:
        def sb(name, shape, dt):
            return es.enter_context(nc.sbuf_tensor(name, list(shape), dt))

        xs2 = sb("xs2", [128, 2, 4, D], F32)
        halo = sb("halo", [128, D], F32)
        xTA = sb("xTA", [128, 8, T], BF16)
        xTB = sb("xTB", [128, 8, T], BF16)
        mo = sb("mo", [128, 8, T], BF16)
        R = sb("R", [128, NFC * T], BF16)
        kvb = sb("kvb", [128, 16384], BF16)
        qkt = sb("qkt", [128, 8, T], BF16)
        ring = sb("ring", [128, NSLOT, 2048], BF16)
        wo_sb = sb("wo_sb", [128, 8, D], BF16)
        gbuf = sb("gbuf", [128, 2, 2 * D], F32)
        ident = sb("ident", [128, 128], F32)
        tri = sb("tri", [128, 4, 128], BF16)
        mask = sb("mask", [128, 4, 512], BF16)
        band = sb("band", [128, 12, 128], BF16)
        wp = sb("wp", [128, 8, 256], BF16)
        gbc = sb("gbc_sb", [128, 8, 8], F32)
        sg = sb("sg", [128, 2, T], F32)
        lnst = sb("lnst", [128, 8, 12], F32)
        lnmv = sb("lnmv", [128, 8, 2], F32)
        lnt = sb("lnt", [128, 8, 4], F32)
        EG = sb("EG", [128, 5, 1024], BF16)
        Eb = EG[:, 0:3, :]
        Gb = EG[:, 3:5, :]
        vt = EG[:, 0:4, :]
        vt_cell = [("E", 0), ("E", 1), ("E", 2), ("G", 0)]
        KP = sb("KP", [128, 5, 1024], BF16)
        Pb = KP[:, 0:3, :]
        ab = KP[:, 3:5, :]
        kst = KP[:, 0:4, :].rearrange("p a (c t) -> p (a c) t", t=T)
        kst_cell = [("P", 0), ("P", 0), ("P", 1), ("P", 1), ("P", 2), ("P", 2), ("a", 0), ("a", 0)]
        ps = es.enter_context(nc.psum_tensor("ps", [128, 8, 512], F32))

        hT = R[:, 0:NFC * T].rearrange("p (c t) -> p c t", t=T)
        xb = R[:, 0:5 * D].rearrange("p (b d) -> p b d", d=D)
        kbuf = [kvb[:, 0:4096], kvb[:, 4096:8192]]
        vbuf = [kvb[:, 8192:12288].rearrange("p (k d) -> p k d", d=128),
                kvb[:, 12288:16384].rearrange("p (k d) -> p k d", d=128)]

        def hT_cells(fc):
            return [("R", fc)]

        def xb_cells(b):
            return [("R", 2 * b), ("R", 2 * b + 1)]

        kbuf_cells = [[("kb", 0)], [("kb", 1)]]
        vbuf_cells = [[("vb", 0)], [("vb", 1)]]

        def xsc(par, b):
            return [("xs", par, b)]

        sc.dma("sp", "c0", lambda e: e.dma_start(out=ident[:], in_=c_ident), writes=["ident"])
        sc.dma("sp", "c1", lambda e: e.dma_start(out=tri[:], in_=c_tri), writes=["tri"])
        sc.dma("sp", "c2", lambda e: e.dma_start(out=mask[:], in_=c_mask), writes=["mask"])
        sc.dma("sp", "c3", lambda e: e.dma_start(out=band[:], in_=c_band), writes=["band"])
        sc.dma("sp", "c4", lambda e: e.dma_start(out=gbc[:], in_=gbc_in), writes=["gbc"])
        wpf = xs2[:, 1, 0:2, :].rearrange("p a (k d) -> p (a k) d", d=256)
        sc.dma("sp", "c5", lambda e: e.dma_start(
            out=wpf, in_=pool_w.rearrange("g (k p) d -> p (g k) d", p=128)),
            writes=xsc(1, 0) + xsc(1, 1))
        sc.dma("sp", "c6", lambda e: e.dma_start(out=halo[:], in_=pool_scale.partition_broadcast(128)),
               writes=["halo"])
        for g in range(4):
            for k in range(2):
                sc.op("dve", lambda e, g=g, k=k: e.tensor_tensor(
                    wp[:, 2 * g + k, :], wpf[:, 2 * g + k, :], halo[:, 256 * g:256 * (g + 1)], ALU.mult),
                    reads=xsc(1, 0) + xsc(1, 1) + ["halo"], writes=[("wp", 2 * g + k)])
        wp_cells = [("wp", j) for j in range(8)]

        def prep_wgu(l):
            g_src = w_gate[l].rearrange("(kc p) (fc f) -> fc p kc f", p=128, f=128)
            u_src = w_up[l].rearrange("(kc p) (fc f) -> fc p kc f", p=128, f=128)
            for fc in range(NFC):
                grp = fc if l == 0 else (0 if fc < 4 else (1 if fc < 12 else 2))
                cell = ("wgu_s", l, grp)
                sc.dma("pool", f"pwgu{l}_{grp}", lambda e, fc=fc: e.dma_start(
                    out=wgu_s[l][fc][:, 0], in_=g_src[fc]), writes=[cell])
                sc.dma("pool", f"pwgu{l}_{grp}", lambda e, fc=fc: e.dma_start(
                    out=wgu_s[l][fc][:, 1], in_=u_src[fc]), writes=[cell])

        def wgu_cell(l, fc):
            return ("wgu_s", l, fc if l == 0 else (0 if fc < 4 else (1 if fc < 12 else 2)))

        def prep_wd(l):
            for h in range(2):
                sc.dma("pool", f"pwd{l}", lambda e, h=h: e.dma_start(
                    out=wd_s[l][h * 1408:(h + 1) * 1408, :], in_=w_down[l][h * 1408:(h + 1) * 1408, :]),
                    writes=[("wd_s", l)])

        def prep_attn():
            for oc in range(16):
                sc.dma("pool", "pwqk", lambda e, oc=oc: e.dma_start(
                    out=wqk_s[oc], in_=w_qkv[:, oc * 128:(oc + 1) * 128].rearrange("(kc p) f -> p kc f", p=128)),
                    writes=["wqk_s"])
            sc.dma("pool", "pwv", lambda e: e.dma_start(out=wv_s, in_=w_qkv[:, 2048:3072]), writes=["wv_s"])
            sc.dma("pool", "pwo", lambda e: e.dma_start(
                out=wo_sb[:, :, :], in_=w_o.rearrange("(kc p) d -> p kc d", p=128)), writes=["wo_sb"])

        ring_state = {"n": 0}

        def ring_load(src_ap, view_fn, src_cells):
            s = ring_state["n"] % NSLOT
            ring_state["n"] += 1
            dst = view_fn(ring[:, s, :])
            cell = ("ring", s)
            sc.dma("sp", f"ring{s}", lambda e: e.dma_start(out=dst, in_=src_ap),
                   reads=src_cells, writes=[cell])
            return dst, [cell]

        gb_state = {"n": 0}

        def gb_load(kidx):
            s = gb_state["n"] % 2
            gb_state["n"] += 1
            sc.dma("sp", f"gb{s}", lambda e: e.dma_start(
                out=gbuf[:, s, :], in_=gb[2 * D * kidx:2 * D * (kidx + 1)].partition_broadcast(128)),
                writes=[("gb", s)])
            return s

        def bank(k):
            return ps[:, k, :]

        def dbank(k):
            return ps[:, 2 * k:2 * k + 2, :].rearrange("p a n -> p (a n)")

        def pcell(k):
            return [("ps", k)]

        def dpcell(k):
            return [("ps", 2 * k), ("ps", 2 * k + 1)]

        def ln_gen(par, b, gs, st):
            xc = xsc(par, b)
            q = 4 * st + b
            xrow = xs2[:, par, b, :]
            sc.op("dve", lambda e: e.bn_stats(lnst[:, q, 0:6], xs2[:, par, b, 0:512]), reads=xc, writes=[("ln", q)])
            sc.op("dve", lambda e: e.bn_stats(lnst[:, q, 6:12], xs2[:, par, b, 512:1024]), reads=xc, writes=[("ln2", q)])
            sc.op("dve", lambda e: e.bn_aggr(lnmv[:, q, :], lnst[:, q, :].rearrange("p (j t) -> p j t", t=6)),
                  reads=[("ln", q), ("ln2", q)], writes=[("lnmv", q)])
            sc.op("dve", lambda e: e.tensor_scalar(lnt[:, q, 0:1], lnmv[:, q, 1:2], EPS, None, ALU.add),
                  reads=[("lnmv", q)], writes=[("lnt0", q)])
            yield
            sc.op("act", lambda e: e.activation(lnt[:, q, 1:2], lnt[:, q, 0:1], AF.Ln),
                  reads=[("lnt0", q)], writes=[("lnt1", q)])
            sc.op("act", lambda e: e.activation(lnt[:, q, 2:3], lnt[:, q, 1:2], AF.Exp, scale=-0.5),
                  reads=[("lnt1", q)], writes=[("lnt2", q)])
            yield
            sc.op("dve", lambda e: e.tensor_scalar(lnt[:, q, 3:4], lnmv[:, q, 0:1], lnt[:, q, 2:3], -1.0,
                                                   ALU.mult, ALU.mult),
                  reads=[("lnmv", q), ("lnt2", q)], writes=[("lnt3", q)])
            yield
            sc.op("act", lambda e: e.activation(xrow, xrow, AF.Identity, bias=lnt[:, q, 3:4], scale=lnt[:, q, 2:3]),
                  reads=xc + [("lnt2", q), ("lnt3", q)], writes=xc)
            yield
            sc.op("dve", lambda e: e.tensor_tensor(xrow, xrow, gbuf[:, gs, 0:D], ALU.mult),
                  reads=xc + [("gb", gs)], writes=xc)
            yield
            beng = "dve" if (sc.epoch == 0 or b % 2 == 1) else "pool"
            sc.op(beng, lambda e: e.tensor_tensor(xrow, xrow, gbuf[:, gs, D:2 * D], ALU.add),
                  reads=xc + [("gb", gs)], writes=xc)
            yield

        def ln_fold_gen(par, b, gs, st, kidx, xT, tag):
            xc = xsc(par, b)
            q = 4 * st + b
            xrow = xs2[:, par, b, :]
            sc.op("dve", lambda e: e.bn_stats(lnst[:, q, 0:6], xs2[:, par, b, 0:512]), reads=xc, writes=[("ln", q)])
            sc.op("dve", lambda e: e.bn_stats(lnst[:, q, 6:12], xs2[:, par, b, 512:1024]), reads=xc, writes=[("ln2", q)])
            sc.op("dve", lambda e: e.bn_aggr(lnmv[:, q, :], lnst[:, q, :].rearrange("p (j t) -> p j t", t=6)),
                  reads=[("ln", q), ("ln2", q)], writes=[("lnmv", q)])
            sc.op("dve", lambda e: e.tensor_scalar(lnt[:, q, 0:1], lnmv[:, q, 1:2], EPS, None, ALU.add),
                  reads=[("lnmv", q)], writes=[("lnt0", q)])
            yield
            sc.op("act", lambda e: e.activation(lnt[:, q, 1:2], lnt[:, q, 0:1], AF.Ln),
                  reads=[("lnt0", q)], writes=[("lnt1", q)])
            sc.op("act", lambda e: e.activation(lnt[:, q, 2:3], lnt[:, q, 1:2], AF.Exp, scale=-0.5),
                  reads=[("lnt1", q)], writes=[("lnt2", q)])
            yield
            sc.op("dve", lambda e: e.tensor_scalar(lnt[:, q, 3:4], lnmv[:, q, 0:1], lnt[:, q, 2:3], -1.0,
                                                   ALU.mult, ALU.mult),
                  reads=[("lnmv", q), ("lnt2", q)], writes=[("lnt3", q)])
            yield
            sc.op("act", lambda e: e.activation(xrow, xrow, AF.Identity, bias=lnt[:, q, 3:4], scale=lnt[:, q, 2:3]),
                  reads=xc + [("lnt2", q), ("lnt3", q)], writes=xc)
            yield
            tp = dbank(b).rearrange("p (j t) -> p j t", t=128)
            for j in range(8):
                sc.op("pe", lambda e, j=j: e.transpose(tp[:, j, :], xs2[:, par, b, 128 * j:128 * (j + 1)], ident[:]),
                      reads=xc + ["ident"], writes=dpcell(b), signal=(j == 7))
            for j in range(8):
                dst = xT[:, j, 128 * b:128 * (b + 1)]
                gcol = gbc[:, 2 * kidx, j:j + 1]
                bcol = gbc[:, 2 * kidx + 1, j:j + 1]
                sc.op("act", lambda e, j=j: e.activation(dst, tp[:, j, :], AF.Identity, bias=bcol, scale=gcol),
                      reads=dpcell(b) + ["gbc"], writes=[(tag, b)])
            yield
            sc.op("dve", lambda e: e.tensor_tensor(xrow, xrow, gbuf[:, gs, 0:D], ALU.mult),
                  reads=xc + [("gb", gs)], writes=xc)
            yield
            beng = "dve" if (sc.epoch == 0 or b % 2 == 1) else "pool"
            sc.op(beng, lambda e: e.tensor_tensor(xrow, xrow, gbuf[:, gs, D:2 * D], ALU.add),
                  reads=xc + [("gb", gs)], writes=xc)
            yield

        def ln4_fold(par, gs, st, kidx, xT, tag):
            gens = [ln_fold_gen(par, b, gs, st, kidx, xT, tag) for b in range(4)]
            for _stage in range(7):
                for g in gens:
                    next(g, None)

        def ln4_gen(par, gs, st):
            gens = [ln_gen(par, b, gs, st) for b in range(4)]
            for _stage in range(6):
                for g in gens:
                    next(g, None)
                yield

        def ln4(par, gs, st):
            for _ in ln4_gen(par, gs, st):
                pass

        def transpose_block(par, b, xT, tag):
            k = b
            tp = dbank(k).rearrange("p (j t) -> p j t", t=128)
            for j in range(8):
                sc.op("pe", lambda e, j=j: e.transpose(tp[:, j, :], xs2[:, par, b, 128 * j:128 * (j + 1)], ident[:]),
                      reads=xsc(par, b) + ["ident"], writes=dpcell(k), signal=(j == 7))
            if b % 2 == 0:
                sc.op("act", lambda e: e.activation(xT[:, :, 128 * b:128 * (b + 1)], tp, AF.Copy),
                      reads=dpcell(k), writes=[(tag, b)])
            else:
                sc.op("dve", lambda e: e.tensor_copy(xT[:, :, 128 * b:128 * (b + 1)], tp),
                      reads=dpcell(k), writes=[(tag, b)])

        def ffn0(par):
            xT_cells = [("xTA", b) for b in range(4)]
            for fc in range(NFC):
                slot, scell = ring_load(wgu_s[0][fc], lambda r: r.rearrange("p (a k f) -> p a k f", a=2, k=8),
                                        [wgu_cell(0, fc)])
                kA, kB = 2 * (fc % 4), 2 * (fc % 4) + 1
                for a, kk in ((0, kA), (1, kB)):
                    for kc in range(8):
                        sc.op("pe", lambda e, a=a, kk=kk, kc=kc: e.matmul(
                            bank(kk), slot[:, a, kc, :], xTA[:, kc, :], start=(kc == 0), stop=(kc == 7)),
                            reads=scell + xT_cells, writes=pcell(kk), signal=(kc == 7))
                sc.op("act", lambda e, kA=kA, fc=fc: e.activation(sg[:, fc % 2, :], bank(kA), AF.Silu),
                      reads=pcell(kA), writes=[("sg", fc % 2)])
                sc.op("dve", lambda e, kB=kB, fc=fc: e.tensor_tensor(hT[:, fc, :], sg[:, fc % 2, :], bank(kB), ALU.mult),
                      reads=[("sg", fc % 2)] + pcell(kB), writes=hT_cells(fc))
            gs = gb_load(1)
            for dh in range(2):
                for grp in range(6):
                    fcs = list(range(4 * grp, min(4 * grp + 4, NFC)))
                    nf = len(fcs)
                    src = wd_s[0].rearrange("(fc p) d -> p fc d", p=128)[:, fcs[0]:fcs[0] + nf, dh * 512:(dh + 1) * 512]
                    slot, scell = ring_load(src, lambda r, nf=nf: r[:, 0:nf * 512].rearrange("p (a d) -> p a d", d=512),
                                            [("wd_s", 0)])
                    for fi, fc in enumerate(fcs):
                        for b in range(4):
                            kk = 4 * dh + b
                            sc.op("pe", lambda e, fi=fi, fc=fc, b=b, kk=kk: e.matmul(
                                bank(kk), hT[:, fc, 128 * b:128 * (b + 1)], slot[:, fi, :],
                                start=(fc == 0), stop=(fc == NFC - 1)),
                                reads=scell + hT_cells(fc), writes=pcell(kk), signal=(fc == NFC - 1 or fi == nf - 1))
                for b in range(4):
                    kk = 4 * dh + b
                    sc.op("dve", lambda e, b=b, kk=kk: e.scalar_tensor_tensor(
                        xs2[:, par, b, dh * 512:(dh + 1) * 512], xs2[:, par, b, dh * 512:(dh + 1) * 512], ALPHA, bank(kk),
                        ALU.mult, ALU.add),
                        reads=xsc(par, b) + pcell(kk), writes=xsc(par, b))
            ln4_fold(par, gs, 0, 1, xTA, "xTA")

        def ffn1_gen(ti):
            par = ti % 2
            xT_cells = [("xTB", b) for b in range(4)]
            pend = []
            for fc in range(NFC):
                slot, scell = ring_load(wgu_s[1][fc], lambda r: r.rearrange("p (a k f) -> p a k f", a=2, k=8),
                                        [wgu_cell(1, fc)])
                kA = 5 if fc % 2 == 0 else 7
                kB = 6
                for a, kk in ((0, kA), (1, kB)):
                    for half in range(2):
                        for kc in range(4 * half, 4 * half + 4):
                            sc.op("pe", lambda e, a=a, kk=kk, kc=kc: e.matmul(
                                bank(kk), slot[:, a, kc, :], xTB[:, kc, :], start=(kc == 0), stop=(kc == 7)),
                                reads=scell + xT_cells, writes=pcell(kk), signal=(kc == 7))
                        if pend:
                            pend.pop(0)()
                        yield

                def epi_act(kA=kA, fc=fc):
                    sc.op("act", lambda e: e.activation(sg[:, fc % 2, :], bank(kA), AF.Exp, scale=-1.0),
                          reads=pcell(kA), writes=[("sg", fc % 2)])

                def epi_dve(kA=kA, kB=kB, fc=fc):
                    sgc = [("sg", fc % 2)]
                    sc.op("dve", lambda e: e.tensor_scalar(sg[:, fc % 2, :], sg[:, fc % 2, :], 1.0, None, ALU.add),
                          reads=sgc, writes=sgc)
                    sc.op("dve", lambda e: e.reciprocal(sg[:, fc % 2, :], sg[:, fc % 2, :]),
                          reads=sgc, writes=sgc)
                    sc.op("dve", lambda e: e.tensor_tensor(sg[:, fc % 2, :], sg[:, fc % 2, :], bank(kB), ALU.mult),
                          reads=sgc + pcell(kB), writes=sgc)
                    sc.op("dve", lambda e: e.tensor_tensor(hT[:, fc, :], sg[:, fc % 2, :], bank(kA), ALU.mult),
                          reads=sgc + pcell(kA), writes=hT_cells(fc))
                pend = [epi_act, epi_dve]
            while pend:
                pend.pop(0)()
                yield
            gs = gb_load(3)
            for dh in range(2):
                for bp in range(2):
                    banks = (5, 7)
                    for grp in range(6):
                        fcs = list(range(4 * grp, min(4 * grp + 4, NFC)))
                        nf = len(fcs)
                        src = wd_s[1].rearrange("(fc p) d -> p fc d", p=128)[:, fcs[0]:fcs[0] + nf, dh * 512:(dh + 1) * 512]
                        slot, scell = ring_load(src, lambda r, nf=nf: r[:, 0:nf * 512].rearrange("p (a d) -> p a d", d=512),
                                                [("wd_s", 1)])
                        for fi, fc in enumerate(fcs):
                            for bi in range(2):
                                b = 2 * bp + bi
                                kk = banks[bi]
                                sc.op("pe", lambda e, fi=fi, fc=fc, b=b, kk=kk: e.matmul(
                                    bank(kk), hT[:, fc, 128 * b:128 * (b + 1)], slot[:, fi, :],
                                    start=(fc == 0), stop=(fc == NFC - 1)),
                                    reads=scell + hT_cells(fc), writes=pcell(kk),
                                    signal=(fc == NFC - 1 or fi == nf - 1))
                            if fi % 2 == 1 or fi == nf - 1:
                                yield
                    for bi in range(2):
                        b = 2 * bp + bi
                        kk = banks[bi]
                        sc.op("dve", lambda e, b=b, kk=kk: e.scalar_tensor_tensor(
                            xs2[:, par, b, dh * 512:(dh + 1) * 512], xs2[:, par, b, dh * 512:(dh + 1) * 512], ALPHA,
                            bank(kk), ALU.mult, ALU.add),
                            reads=xsc(par, b) + pcell(kk), writes=xsc(par, b))
                    yield
            yield from ln4_gen(par, gs, 1)
            ev = sc.dma("sp", "ost", lambda e: e.dma_start(
                out=out.rearrange("(b p) d -> p b d", p=128)[:, 4 * ti:4 * ti + 4, :], in_=xs2[:, par, :, :]),
                reads=[c for b in range(4) for c in xsc(par, b)])
            out_evs.append(ev)
            yield

        out_evs = []
        filler = None

        def fill(n=1):
            nonlocal filler
            for _ in range(n):
                if filler is None:
                    return
                try:
                    next(filler)
                except StopIteration:
                    filler = None
                    return

        def flush():
            while filler is not None:
                fill(1)

        gs0_of = {}

        def prefetch_x(ti):
            par = ti % 2
            r0 = T * ti
            sc.dma("sp", "xld", lambda e: e.dma_start(
                out=xs2[:, par, :, :], in_=x.rearrange("(b p) d -> p b d", p=128)[:, 4 * ti:4 * ti + 4, :]),
                writes=[c for b in range(4) for c in xsc(par, b)])
            if ti == 0:
                sc.op("pool", lambda e: e.memset(halo[:], 0.0), writes=["halo"])
            else:
                sc.dma("sp", "hld", lambda e: e.dma_start(out=halo[:], in_=x[r0 - 128:r0, :]),
                       writes=["halo"])
            gs0_of[ti] = gb_load(0)
            sc.op("pool", lambda e: e.tensor_copy(xb[:, 0, :], halo[:]), reads=["halo"], writes=xb_cells(0))
            if ti == 0 and PREP:
                prep_wgu(0)
                prep_wd(0)
                prep_attn()
                prep_wgu(1)
                prep_wd(1)
            for b in range(4):
                if b < 2:
                    sc.op("act", lambda e, b=b: e.activation(xb[:, b + 1, :], xs2[:, par, b, :], AF.Copy),
                          reads=xsc(par, b), writes=xb_cells(b + 1))
                else:
                    sc.op("dve", lambda e, b=b: e.tensor_copy(xb[:, b + 1, :], xs2[:, par, b, :]),
                          reads=xsc(par, b), writes=xb_cells(b + 1))

        for i in range(NT):
            sc.epoch = i
            r0 = T * i
            par = i % 2
            if i == 0:
                prefetch_x(0)
            gs0 = gs0_of[i]
            for j in range(8):
                g = j // 2
                kk = j % 8
                for b in range(4):
                    cur = (8 + g) if (i == 0 and b == 0) else g
                    sc.op("pe", lambda e, j=j, b=b, cur=cur, kk=kk: e.matmul(
                        ps[:, kk, 128 * b:128 * (b + 1)], xb[:, b + 1, 128 * j:128 * (j + 1)], band[:, cur, :],
                        start=True, stop=False),
                        reads=xb_cells(b + 1) + ["band"], writes=pcell(kk), signal=False)
                    sc.op("pe", lambda e, j=j, b=b, g=g, kk=kk: e.matmul(
                        ps[:, kk, 128 * b:128 * (b + 1)], xb[:, b, 128 * j:128 * (j + 1)], band[:, 4 + g, :],
                        start=False, stop=True),
                        reads=xb_cells(b) + ["band"], writes=pcell(kk), signal=(b == 3))
                if j % 2 == 0:
                    sc.op("act", lambda e, j=j, kk=kk: e.activation(mo[:, j, :], bank(kk), AF.Copy),
                          reads=pcell(kk), writes=[("mo", j)])
                else:
                    sc.op("dve", lambda e, j=j, kk=kk: e.tensor_copy(mo[:, j, :], bank(kk)),
                          reads=pcell(kk), writes=[("mo", j)])
            for b in range(4):
                for g in range(4):
                    for k in range(2):
                        sc.op("pe", lambda e, b=b, g=g, k=k: e.matmul(
                            ps[:, 2 * b + g // 2, (g % 2) * 256:(g % 2) * 256 + 256],
                            mo[:, 2 * g + k, 128 * b:128 * (b + 1)], wp[:, 2 * g + k, :],
                            start=(k == 0), stop=(k == 1)),
                            reads=[("mo", 2 * g + k)] + wp_cells, writes=dpcell(b), signal=(g == 3 and k == 1))
                sc.op("dve", lambda e, b=b: e.scalar_tensor_tensor(
                    xs2[:, par, b, :], xs2[:, par, b, :], ALPHA, dbank(b), ALU.mult, ALU.add),
                    reads=xsc(par, b) + dpcell(b), writes=xsc(par, b))
            ln4_fold(par, gs0, 0, 0, xTA, "xTA")
            ffn0(par)
            xT_cells = [("xTA", b) for b in range(4)]
            for op2 in range(8):
                slot, scell = ring_load(wqk_s[2 * op2:2 * op2 + 2].rearrange("o p k f -> p o k f"),
                                        lambda r: r.rearrange("p (a k f) -> p a k f", a=2, k=8), ["wqk_s"])
                for a in range(2):
                    oc = 2 * op2 + a
                    kk = oc % 8
                    for kc in range(8):
                        sc.op("pe", lambda e, a=a, kk=kk, kc=kc: e.matmul(
                            bank(kk), slot[:, a, kc, :], xTA[:, kc, :], start=(kc == 0), stop=(kc == 7)),
                            reads=scell + xT_cells, writes=pcell(kk), signal=(kc == 7))
                    dst = qkt[:, oc, :] if oc < 8 else kst[:, oc - 8, :]
                    dcell = [("qkt", oc)] if oc < 8 else [kst_cell[oc - 8]]
                    if oc % 2 == 0:
                        sc.op("act", lambda e, kk=kk: e.activation(dst, bank(kk), AF.Copy),
                              reads=pcell(kk), writes=dcell)
                    else:
                        sc.op("dve", lambda e, kk=kk: e.tensor_copy(dst, bank(kk)),
                              reads=pcell(kk), writes=dcell)
            for dh in range(2):
                for grp in range(2):
                    src = wv_s.rearrange("(kc p) d -> p kc d", p=128)[:, 4 * grp:4 * grp + 4, dh * 512:(dh + 1) * 512]
                    slot, scell = ring_load(src, lambda r: r.rearrange("p (a d) -> p a d", d=512), ["wv_s"])
                    for ki in range(4):
                        kc = 4 * grp + ki
                        for b in range(4):
                            kk = 4 * dh + b
                            sc.op("pe", lambda e, ki=ki, kc=kc, b=b, kk=kk: e.matmul(
                                bank(kk), xTA[:, kc, 128 * b:128 * (b + 1)], slot[:, ki, :],
                                start=(kc == 0), stop=(kc == 7)),
                                reads=scell + [("xTA", b)], writes=pcell(kk), signal=(kc == 7 or ki == 3))
                for b in range(4):
                    kk = 4 * dh + b
                    if b % 2 == 0:
                        sc.op("act", lambda e, b=b, kk=kk, dh=dh: e.activation(
                            vt[:, b, dh * 512:(dh + 1) * 512], bank(kk), AF.Copy),
                            reads=pcell(kk), writes=[vt_cell[b]])
                    else:
                        sc.op("dve", lambda e, b=b, kk=kk, dh=dh: e.tensor_copy(
                            vt[:, b, dh * 512:(dh + 1) * 512], bank(kk)),
                            reads=pcell(kk), writes=[vt_cell[b]])
            sc.dma("sp", "kw", lambda e: e.dma_start(
                out=kt_s.rearrange("h p s -> p h s")[:, :, r0:r0 + T], in_=kst),
                reads=[("P", 0), ("P", 1), ("P", 2), ("a", 0)], writes=["kt_s"])
            sc.dma("sp", "vw", lambda e: e.dma_start(
                out=v_s.rearrange("(b p) d -> p b d", p=128)[:, 4 * i:4 * i + 4, :], in_=vt[:, :, :]),
                reads=vt_cell, writes=["v_s"])

            nkb = 4 * i + 4
            nkeys = 128 * nkb
            steps = [(hp, k) for hp in range(8) for k in range(nkb)]
            nst = len(steps)

            def kv_load(hp):
                pb = hp % 2
                sc.dma("sp", f"kld{pb}", lambda e: e.dma_start(
                    out=kbuf[pb][:, 0:nkeys], in_=kt_s[hp][:, 0:nkeys]),
                    reads=["kt_s"], writes=kbuf_cells[pb])
                sc.dma("sp", f"vld{pb}", lambda e: e.dma_start(
                    out=vbuf[pb][:, 0:nkb, :],
                    in_=v_s.rearrange("(k p) d -> p k d", p=128)[:, 0:nkb, 128 * hp:128 * (hp + 1)]),
                    reads=["v_s"], writes=vbuf_cells[pb])

            def c0_of(m):
                hp, k = steps[m]
                kb = nkb - 1 - k
                return 128 * (kb - 4 * i) if kb >= 4 * i else 0

            def v3(ap2, c0):
                return ap2.rearrange("p (l n) -> p l n", l=2)[:, :, c0:512]

            def st_Z(m):
                hp, k = steps[m]
                kb = nkb - 1 - k
                diag = kb >= 4 * i
                jj = kb - 4 * i
                c0 = c0_of(m)
                for l in range(2):
                    sc.op("pe", lambda e, l=l: e.matmul(
                        ps[:, l, c0:512], kbuf[hp % 2][64 * l:64 * l + 64, 128 * kb:128 * (kb + 1)],
                        qkt[64 * l:64 * l + 64, hp, c0:512], start=True, stop=(not diag)),
                        reads=kbuf_cells[hp % 2] + [("qkt", hp)], writes=dpcell(0),
                        signal=(l == 1 and not diag))
                if diag:
                    for l in range(2):
                        sc.op("pe", lambda e, l=l: e.matmul(
                            ps[:, l, c0:512], tri[:, 2, :], mask[:, jj, c0:512], start=False, stop=True),
                            reads=["tri", "mask"], writes=dpcell(0), signal=(l == 1))

            def st_EP(m):
                c0 = c0_of(m)
                sc.op("act", lambda e: e.activation(v3(Eb[:, m % 3, :], c0), ps[:, 0:2, c0:512], AF.Exp, scale=0.125),
                      reads=dpcell(0), writes=[("E", m % 3)])
                sc.op("act", lambda e: e.activation(v3(Pb[:, m % 3, :], c0), v3(Eb[:, m % 3, :], c0), AF.Ln, bias=1.0),
                      reads=[("E", m % 3)], writes=[("P", m % 3)])

            def st_C(m):
                hp, k = steps[m]
                c0 = c0_of(m)
                for l in range(2):
                    if k == 0:
                        sc.op("pe", lambda e, l=l: e.matmul(
                            ps[:, 2 + l, :], tri[:, 3, :], mask[:, 0, :], start=True, stop=True,
                            skip_group_check=True),
                            reads=["tri", "mask"], writes=dpcell(1), signal=False)
                    else:
                        cp = c0_of(m - 1)
                        sc.op("pe", lambda e, l=l: e.matmul(
                            ps[:, 2 + l, cp:512], tri[:, 1, :], Pb[:, (m - 1) % 3, 512 * l + cp:512 * (l + 1)],
                            start=False, stop=False, skip_group_check=True),
                            reads=[("P", (m - 1) % 3), "tri"], writes=dpcell(1), signal=False)
                    sc.op("pe", lambda e, l=l: e.matmul(
                        ps[:, 2 + l, c0:512], tri[:, 0, :], Pb[:, m % 3, 512 * l + c0:512 * (l + 1)],
                        start=False, stop=True, skip_group_check=True),
                        reads=[("P", m % 3), "tri"], writes=dpcell(1), signal=(l == 1))

            def st_G(m):
                c0 = c0_of(m)
                sc.op("act", lambda e: e.activation(v3(Gb[:, m % 2, :], c0), ps[:, 2:4, c0:512], AF.Exp, scale=-1.0),
                      reads=dpcell(1), writes=[("G", m % 2)])

            def st_A(m):
                c0 = c0_of(m)
                sc.op("dve", lambda e: e.tensor_tensor(v3(ab[:, m % 2, :], c0), v3(Eb[:, m % 3, :], c0),
                                                       v3(Gb[:, m % 2, :], c0), ALU.mult),
                      reads=[("E", m % 3), ("G", m % 2)], writes=[("a", m % 2)])

            def st_O(m):
                hp, k = steps[m]
                kb = nkb - 1 - k
                c0 = c0_of(m)
                for l in range(2):
                    if k == 0:
                        sc.op("pe", lambda e, l=l: e.matmul(
                            ps[64 * l:64 * l + 64, 4, :], tri[:, 3, 0:64], mask[:, 0, :], start=True, stop=False,
                            skip_group_check=True),
                            reads=["tri", "mask"], writes=pcell(4), signal=False)
                    sc.op("pe", lambda e, l=l: e.matmul(
                        ps[64 * l:64 * l + 64, 4, c0:512], vbuf[hp % 2][:, kb, 64 * l:64 * l + 64],
                        ab[:, m % 2, 512 * l + c0:512 * (l + 1)], start=False, stop=(k == nkb - 1),
                        skip_group_check=True),
                        reads=vbuf_cells[hp % 2] + [("a", m % 2)], writes=pcell(4), signal=(l == 1))
                if k == nkb - 1:
                    sc.op("dve", lambda e: e.tensor_copy(mo[:, hp, :], bank(4)),
                          reads=pcell(4), writes=[("mo", hp)])

            frate = 0.0 if i == 1 else min(2.0, FR / nst)
            facc = 0.0
            kv_load(0)
            st_Z(0)
            for m in range(nst + 2):
                if m < nst:
                    hp, k = steps[m]
                    if k == 2 and hp + 1 < 8:
                        kv_load(hp + 1)
                    st_EP(m)
                if 0 <= m - 1 < nst:
                    st_C(m - 1)
                    st_G(m - 1)
                    st_A(m - 1)
                if m + 1 < nst:
                    st_Z(m + 1)
                    for _ in range(DUMMY):
                        st_Z(m + 1)
                if 0 <= m - 2 < nst:
                    st_O(m - 2)
                facc += frate
                while facc >= 1.0:
                    fill(1)
                    facc -= 1.0

            gs2 = gb_load(2)
            lns = []

            def advance_lns(limit=4):
                for g in lns:
                    if g[1] < limit:
                        next(g[0], None)
                        g[1] += 1

            for b in range(4):
                for dh in range(2):
                    kk = (2 * b + dh) % 5
                    for kc in range(8):
                        sc.op("pe", lambda e, kc=kc, b=b, dh=dh, kk=kk: e.matmul(
                            bank(kk), mo[:, kc, 128 * b:128 * (b + 1)], wo_sb[:, kc, dh * 512:(dh + 1) * 512],
                            start=(kc == 0), stop=(kc == 7)),
                            reads=["wo_sb", ("mo", kc)], writes=pcell(kk), signal=(kc == 7))
                    sc.op("dve", lambda e, b=b, dh=dh, kk=kk: e.scalar_tensor_tensor(
                        xs2[:, par, b, dh * 512:(dh + 1) * 512], xs2[:, par, b, dh * 512:(dh + 1) * 512], ALPHA, bank(kk),
                        ALU.mult, ALU.add),
                        reads=xsc(par, b) + pcell(kk), writes=xsc(par, b))
                    advance_lns()
                lns.append([ln_fold_gen(par, b, gs2, 0, 2, xTB, "xTB"), 0])
                fill(1)
            for _ in range(4):
                advance_lns(4)
                fill(1)
            flush()
            for _ in range(3):
                advance_lns(7)
            if i + 1 < NT:
                prefetch_x(i + 1)
            filler = ffn1_gen(i)
        flush()
        sc.wait_only("sp", out_evs)

        sems = {}
        for key in sorted(sc.keys, key=str):
            sems[key] = es.enter_context(nc.semaphore("s_" + "_".join(str(t) for t in key)))
        sc.replay(nc, sems)
    return nc


def make_consts():
    bf = ml_dtypes.bfloat16
    ident = np.eye(128, dtype=np.float32)
    j = np.arange(128)[:, None]
    s = np.arange(128)[None, :]
    tri = np.zeros((128, 4, 128), np.float32)
    tri[:, 0, :] = (j >= s)
    tri[:, 1, :] = (j < s)
    tri[:, 2, :] = -2048.0 * (j == s)
    mask = np.zeros((128, 4, 512), np.float32)
    t = np.arange(512)[None, :]
    for jj in range(4):
        mask[:, jj, :] = ((128 * jj + np.arange(128)[:, None]) >= t)
    band = np.zeros((128, 12, 128), np.float32)
    tp = np.arange(128)[:, None]
    tt = np.arange(128)[None, :]
    for g, w in enumerate(WINDOWS):
        cur = ((tp <= tt) & (tp > tt - w)).astype(np.float32) / w
        band[:, g, :] = cur - (tp == tt)
        band[:, 4 + g, :] = ((tp - 128) > (tt - w)).astype(np.float32) / w
        cnt = np.minimum(tt + 1, w).astype(np.float32)
        band[:, 8 + g, :] = ((tp <= tt) & (tp > tt - w)).astype(np.float32) / cnt - (tp == tt)
    return {"c_ident": ident, "c_tri": tri.astype(bf), "c_mask": mask.astype(bf), "c_band": band.astype(bf)}


def make_in_maps(x, ln_mix_g, ln_mix_b, ln_ffn_g, ln_ffn_b, pool_w, pool_scale, w_qkv, w_o, w_gate, w_up, w_down):
    f = lambda a: np.ascontiguousarray(np.asarray(a, dtype=np.float32))
    gb = np.stack([f(ln_mix_g)[0], f(ln_mix_b)[0], f(ln_ffn_g)[0], f(ln_ffn_b)[0],
                   f(ln_mix_g)[1], f(ln_mix_b)[1], f(ln_ffn_g)[1], f(ln_ffn_b)[1]], axis=0).reshape(-1)
    shared = {
        "gb": np.ascontiguousarray(gb), "pool_w": f(pool_w)[0], "pool_scale": f(pool_scale)[0],
        "w_qkv": f(w_qkv)[0], "w_o": f(w_o)[0], "w_gate": f(w_gate), "w_up": f(w_up), "w_down": f(w_down),
    }
    shared["gbc"] = np.ascontiguousarray(gb.reshape(8, 8, 128).transpose(2, 0, 1))
    shared.update(make_consts())
    x = f(x)
    return [dict(shared, x=x[c]) for c in range(8)]


def kernel(x, ln_mix_g, ln_mix_b, ln_ffn_g, ln_ffn_b, pool_w, pool_scale, w_qkv, w_o, w_gate, w_up, w_down):
    in_maps = make_in_maps(x, ln_mix_g, ln_mix_b, ln_ffn_g, ln_ffn_b, pool_w, pool_scale,
                           w_qkv, w_o, w_gate, w_up, w_down)
    nc = build(NTILES)
    res = run_bass_kernel_spmd(nc, in_maps, core_ids=list(range(8)))
    return np.stack([np.asarray(r["out"]).astype(np.float32) for r in res.results], axis=0)
```

```python
import math
import types
from contextlib import ExitStack

import numpy as np
import ml_dtypes
import concourse.bass as bass
import concourse.mybir as mybir
from concourse.bass_utils import run_bass_kernel_spmd

F32 = mybir.dt.float32
BF16 = mybir.dt.bfloat16
AF = mybir.ActivationFunctionType
ALU = mybir.AluOpType

D = 1024
FF = 2816
NFC = FF // 128
S = 4096
T = 512
NTILES = S // T
ALPHA = float((2.0 * 2) ** 0.25)
EPS = 1e-5
NSLOT = 4
WINDOWS = (2, 4, 8, 16)


def _snap(fn):
    if fn is None or fn.__closure__ is None:
        return fn
    cells = []
    for c in fn.__closure__:
        try:
            cells.append(types.CellType(c.cell_contents))
        except ValueError:
            cells.append(c)
    return types.FunctionType(fn.__code__, fn.__globals__, fn.__name__, fn.__defaults__, tuple(cells))


class Sched:
    ENGS = ("pe", "act", "dve", "pool", "sp")

    def __init__(self):
        self.q = {e: [] for e in self.ENGS}
        self.cnt = {}
        self.epoch = 0
        self.lastw = {}
        self.readers = {}
        self.dmacnt = {}
        self.keys = set()
        self.pending = {e: ([], []) for e in self.ENGS}

    def _deps(self, reads, writes):
        waits = set()
        for c in reads:
            if c in self.lastw:
                waits.add(self.lastw[c])
        for c in writes:
            if c in self.lastw:
                waits.add(self.lastw[c])
            for r in self.readers.get(c, ()):
                waits.add(r)
        return waits

    def _commit(self, ev, reads, writes):
        for c in reads:
            self.readers.setdefault(c, []).append(ev)
        for c in writes:
            self.lastw[c] = ev
            self.readers[c] = []

    def op(self, eng, fn, reads=(), writes=(), signal=True):
        fn = _snap(fn)
        waits = self._deps(reads, writes)
        pr, pw = self.pending[eng]
        pr.extend(reads)
        pw.extend(writes)
        if signal:
            key = (eng, self.epoch)
            self.cnt[key] = self.cnt.get(key, 0) + 1
            ev = (key, self.cnt[key])
            self.keys.add(key)
            self.q[eng].append((waits, fn, (key, 1)))
            self._commit(ev, pr, pw)
            self.pending[eng] = ([], [])
            return ev
        self.q[eng].append((waits, fn, None))
        return None

    def dma(self, eng, sem, fn, reads=(), writes=()):
        fn = _snap(fn)
        waits = self._deps(reads, writes)
        key = ("dma", sem)
        self.dmacnt[key] = self.dmacnt.get(key, 0) + 16
        ev = (key, self.dmacnt[key])
        self.keys.add(key)
        self.q[eng].append((waits, fn, (key, 16)))
        self._commit(ev, reads, writes)
        return ev

    def wait_only(self, eng, evs):
        self.q[eng].append((set(evs), None, None))

    def replay(self, nc, sems):
        handles = {"pe": "tensor", "act": "scalar", "dve": "vector", "pool": "gpsimd", "sp": "sync"}
        with nc.Block() as block:
            for eng in self.ENGS:
                def body(e, eng=eng):
                    waited = {}
                    for waits, fn, sig in self.q[eng]:
                        best = {}
                        for k, v in waits:
                            if eng == "pe" and k[0] == "pe":
                                continue
                            if best.get(k, 0) < v:
                                best[k] = v
                        for k, v in sorted(best.items(), key=lambda kv: str(kv[0])):
                            if waited.get(k, 0) < v:
                                e.wait_ge(sems[k], v)
                                waited[k] = v
                        if fn is None:
                            continue
                        ins = fn(e)
                        if sig is not None:
                            ins.then_inc(sems[sig[0]], sig[1])
                getattr(block, handles[eng])(body)


def build(NT=NTILES, LIM=99, PREP=True, ALIM=99, FILL=4, DUMMY=0, FR=140.0):
    nc = bass.Bass("TRN2", target_bir_lowering=False)
    sc = Sched()

    def din(name, shape, dt=F32):
        return nc.dram_tensor(name, list(shape), dt, kind="ExternalInput").ap()

    def dscr(name, shape, dt=BF16):
        return nc.dram_tensor(name, list(shape), dt, kind="Internal").ap()

    x = din("x", [S, D])
    gb = din("gb", [8 * D])
    pool_w = din("pool_w", [4, 256, 256])
    pool_scale = din("pool_scale", [D])
    w_qkv = din("w_qkv", [D, 3 * D])
    w_o = din("w_o", [D, D])
    w_gu = din("w_gu", [2, 2, D, FF])
    w_down = din("w_down", [2, FF, D])
    c_ident = din("c_ident", [128, 128])
    c_tri = din("c_tri", [128, 4, 128], BF16)
    c_mask = din("c_mask", [128, 4, 512], BF16)
    c_band = din("c_band", [128, 12, 128], BF16)
    out = nc.dram_tensor("out", [S, D], F32, kind="ExternalOutput").ap()

    wgu_s = [dscr(f"wgu_s{l}", [NFC, 128, 2, 8, 128]) for l in range(2)]
    wd_s = [dscr(f"wd_s{l}", [FF, D]) for l in range(2)]
    wqk_s = dscr("wqk_s", [16, 128, 8, 128])
    wv_s = dscr("wv_s", [D, D])
    wo_s = dscr("wo_s", [D, D])
    kt_s = dscr("kt_s", [8, 128, S])
    v_s = dscr("v_s", [S, D])

    es = ExitStack()
    with es:
        def sb(name, shape, dt):
            return es.enter_context(nc.sbuf_tensor(name, list(shape), dt))

        xs2 = sb("xs2", [128, 2, 4, D], F32)
        halo = sb("halo", [128, D], F32)
        xTA = sb("xTA", [128, 8, T], BF16)
        xTB = sb("xTB", [128, 8, T], BF16)
        mo = sb("mo", [128, 8, T], BF16)
        R = sb("R", [128, NFC * T], BF16)
        kvb = sb("kvb", [128, 16384], BF16)
        qkt = sb("qkt", [128, 8, T], BF16)
        ring = sb("ring", [128, NSLOT, 2048], BF16)
        wo_sb = sb("wo_sb", [128, 8, D], BF16)
        gbuf = sb("gbuf", [128, 2, 2 * D], F32)
        ident = sb("ident", [128, 128], F32)
        tri = sb("tri", [128, 4, 128], BF16)
        mask = sb("mask", [128, 4, 512], BF16)
        band = sb("band", [128, 12, 128], BF16)
        wp = sb("wp", [128, 8, 256], BF16)
        sg = sb("sg", [128, 2, T], F32)
        lnst = sb("lnst", [128, 8, 12], F32)
        lnmv = sb("lnmv", [128, 8, 2], F32)
        lnt = sb("lnt", [128, 8, 4], F32)
        EG = sb("EG", [128, 5, 1024], BF16)
        Eb = EG[:, 0:3, :]
        Gb = EG[:, 3:5, :]
        vt = EG[:, 0:4, :]
        vt_cell = [("E", 0), ("E", 1), ("E", 2), ("G", 0)]
        KP = sb("KP", [128, 5, 1024], BF16)
        Pb = KP[:, 0:3, :]
        ab = KP[:, 3:5, :]
        kst = KP[:, 0:4, :].rearrange("p a (c t) -> p (a c) t", t=T)
        kst_cell = [("P", 0), ("P", 0), ("P", 1), ("P", 1), ("P", 2), ("P", 2), ("a", 0), ("a", 0)]
        ps = es.enter_context(nc.psum_tensor("ps", [128, 8, 512], F32))

        hT = R[:, 0:NFC * T].rearrange("p (c t) -> p c t", t=T)
        xb = R[:, 0:5 * D].rearrange("p (b d) -> p b d", d=D)
        kbuf = [kvb[:, 0:4096], kvb[:, 4096:8192]]
        vbuf = [kvb[:, 8192:12288].rearrange("p (k d) -> p k d", d=128),
                kvb[:, 12288:16384].rearrange("p (k d) -> p k d", d=128)]

        def hT_cells(fc):
            return [("R", fc)]

        def xb_cells(b):
            return [("R", 2 * b), ("R", 2 * b + 1)]

        kbuf_cells = [[("kb", 0)], [("kb", 1)]]
        vbuf_cells = [[("vb", 0)], [("vb", 1)]]

        def xsc(par, b):
            return [("xs", par, b)]

        sc.dma("sp", "c0", lambda e: e.dma_start(out=ident[:], in_=c_ident), writes=["ident"])
        sc.dma("sp", "c1", lambda e: e.dma_start(out=tri[:], in_=c_tri), writes=["tri"])
        sc.dma("sp", "c2", lambda e: e.dma_start(out=mask[:], in_=c_mask), writes=["mask"])
        sc.dma("sp", "c3", lambda e: e.dma_start(out=band[:], in_=c_band), writes=["band"])
        wpf = xs2[:, 1, 0:2, :].rearrange("p a (k d) -> p (a k) d", d=256)
        sc.dma("sp", "c5", lambda e: e.dma_start(
            out=wpf, in_=pool_w.rearrange("g (k p) d -> p (g k) d", p=128)),
            writes=xsc(1, 0) + xsc(1, 1))
        sc.dma("sp", "c6", lambda e: e.dma_start(out=halo[:], in_=pool_scale.partition_broadcast(128)),
               writes=["halo"])
        for g in range(4):
            for k in range(2):
                sc.op("dve", lambda e, g=g, k=k: e.tensor_tensor(
                    wp[:, 2 * g + k, :], wpf[:, 2 * g + k, :], halo[:, 256 * g:256 * (g + 1)], ALU.mult),
                    reads=xsc(1, 0) + xsc(1, 1) + ["halo"], writes=[("wp", 2 * g + k)])
        wp_cells = [("wp", j) for j in range(8)]

        def prep_wgu(l):
            src = w_gu[l].rearrange("a (kc p) (fc f) -> fc p (a kc) f", p=128, f=128)
            for fc in range(NFC):
                grp = fc if l == 0 else (0 if fc < 4 else (1 if fc < 12 else 2))
                cell = ("wgu_s", l, grp)
                sc.dma("pool", f"pwgu{l}_{grp}", lambda e, fc=fc: e.dma_start(
                    out=wgu_s[l][fc].rearrange("p a k f -> p (a k) f"), in_=src[fc]), writes=[cell])

        def wgu_cell(l, fc):
            return ("wgu_s", l, fc if l == 0 else (0 if fc < 4 else (1 if fc < 12 else 2)))

        def prep_wd(l):
            for h in range(2):
                sc.dma("pool", f"pwd{l}", lambda e, h=h: e.dma_start(
                    out=wd_s[l][h * 1408:(h + 1) * 1408, :], in_=w_down[l][h * 1408:(h + 1) * 1408, :]),
                    writes=[("wd_s", l)])

        def prep_attn():
            for oc in range(16):
                sc.dma("pool", "pwqk", lambda e, oc=oc: e.dma_start(
                    out=wqk_s[oc], in_=w_qkv[:, oc * 128:(oc + 1) * 128].rearrange("(kc p) f -> p kc f", p=128)),
                    writes=["wqk_s"])
            sc.dma("pool", "pwv", lambda e: e.dma_start(out=wv_s, in_=w_qkv[:, 2048:3072]), writes=["wv_s"])
            sc.dma("pool", "pwo", lambda e: e.dma_start(
                out=wo_sb[:, :, :], in_=w_o.rearrange("(kc p) d -> p kc d", p=128)), writes=["wo_sb"])

        ring_state = {"n": 0}

        def ring_load(src_ap, view_fn, src_cells):
            s = ring_state["n"] % NSLOT
            ring_state["n"] += 1
            dst = view_fn(ring[:, s, :])
            cell = ("ring", s)
            sc.dma("sp", f"ring{s}", lambda e: e.dma_start(out=dst, in_=src_ap),
                   reads=src_cells, writes=[cell])
            return dst, [cell]

        gb_state = {"n": 0}

        def gb_load(kidx):
            s = gb_state["n"] % 2
            gb_state["n"] += 1
            sc.dma("sp", f"gb{s}", lambda e: e.dma_start(
                out=gbuf[:, s, :], in_=gb[2 * D * kidx:2 * D * (kidx + 1)].partition_broadcast(128)),
                writes=[("gb", s)])
            return s

        def bank(k):
            return ps[:, k, :]

        def dbank(k):
            return ps[:, 2 * k:2 * k + 2, :].rearrange("p a n -> p (a n)")

        def pcell(k):
            return [("ps", k)]

        def dpcell(k):
            return [("ps", 2 * k), ("ps", 2 * k + 1)]

        def ln_gen(par, b, gs, st):
            xc = xsc(par, b)
            q = 4 * st + b
            xrow = xs2[:, par, b, :]
            sc.op("dve", lambda e: e.bn_stats(lnst[:, q, 0:6], xs2[:, par, b, 0:512]), reads=xc, writes=[("ln", q)])
            sc.op("dve", lambda e: e.bn_stats(lnst[:, q, 6:12], xs2[:, par, b, 512:1024]), reads=xc, writes=[("ln2", q)])
            sc.op("dve", lambda e: e.bn_aggr(lnmv[:, q, :], lnst[:, q, :].rearrange("p (j t) -> p j t", t=6)),
                  reads=[("ln", q), ("ln2", q)], writes=[("lnmv", q)])
            sc.op("dve", lambda e: e.tensor_scalar(lnt[:, q, 0:1], lnmv[:, q, 1:2], EPS, None, ALU.add),
                  reads=[("lnmv", q)], writes=[("lnt0", q)])
            yield
            sc.op("act", lambda e: e.activation(lnt[:, q, 1:2], lnt[:, q, 0:1], AF.Ln),
                  reads=[("lnt0", q)], writes=[("lnt1", q)])
            sc.op("act", lambda e: e.activation(lnt[:, q, 2:3], lnt[:, q, 1:2], AF.Exp, scale=-0.5),
                  reads=[("lnt1", q)], writes=[("lnt2", q)])
            yield
            sc.op("dve", lambda e: e.tensor_scalar(lnt[:, q, 3:4], lnmv[:, q, 0:1], lnt[:, q, 2:3], -1.0,
                                                   ALU.mult, ALU.mult),
                  reads=[("lnmv", q), ("lnt2", q)], writes=[("lnt3", q)])
            yield
            sc.op("act", lambda e: e.activation(xrow, xrow, AF.Identity, bias=lnt[:, q, 3:4], scale=lnt[:, q, 2:3]),
                  reads=xc + [("lnt2", q), ("lnt3", q)], writes=xc)
            yield
            sc.op("dve", lambda e: e.tensor_tensor(xrow, xrow, gbuf[:, gs, 0:D], ALU.mult),
                  reads=xc + [("gb", gs)], writes=xc)
            yield
            beng = "dve" if (sc.epoch <= 1 or b % 2 == 1) else "pool"
            sc.op(beng, lambda e: e.tensor_tensor(xrow, xrow, gbuf[:, gs, D:2 * D], ALU.add),
                  reads=xc + [("gb", gs)], writes=xc)
            yield

        def ln4_gen(par, gs, st):
            gens = [ln_gen(par, b, gs, st) for b in range(4)]
            for _stage in range(6):
                for g in gens:
                    next(g, None)
                yield

        def ln4(par, gs, st):
            for _ in ln4_gen(par, gs, st):
                pass

        def transpose_block(par, b, xT, tag):
            k = b
            tp = dbank(k).rearrange("p (j t) -> p j t", t=128)
            for j in range(8):
                sc.op("pe", lambda e, j=j: e.transpose(tp[:, j, :], xs2[:, par, b, 128 * j:128 * (j + 1)], ident[:]),
                      reads=xsc(par, b) + ["ident"], writes=dpcell(k), signal=(j == 7))
            if b % 2 == 0:
                sc.op("act", lambda e: e.activation(xT[:, :, 128 * b:128 * (b + 1)], tp, AF.Copy),
                      reads=dpcell(k), writes=[(tag, b)])
            else:
                sc.op("dve", lambda e: e.tensor_copy(xT[:, :, 128 * b:128 * (b + 1)], tp),
                      reads=dpcell(k), writes=[(tag, b)])

        def ffn0(par):
            xT_cells = [("xTA", b) for b in range(4)]
            for fc in range(NFC):
                slot, scell = ring_load(wgu_s[0][fc], lambda r: r.rearrange("p (a k f) -> p a k f", a=2, k=8),
                                        [wgu_cell(0, fc)])
                kA, kB = 2 * (fc % 4), 2 * (fc % 4) + 1
                for a, kk in ((0, kA), (1, kB)):
                    for kc in range(8):
                        sc.op("pe", lambda e, a=a, kk=kk, kc=kc: e.matmul(
                            bank(kk), slot[:, a, kc, :], xTA[:, kc, :], start=(kc == 0), stop=(kc == 7)),
                            reads=scell + xT_cells, writes=pcell(kk), signal=(kc == 7))
                sc.op("act", lambda e, kA=kA, fc=fc: e.activation(sg[:, fc % 2, :], bank(kA), AF.Silu),
                      reads=pcell(kA), writes=[("sg", fc % 2)])
                sc.op("dve", lambda e, kB=kB, fc=fc: e.tensor_tensor(hT[:, fc, :], sg[:, fc % 2, :], bank(kB), ALU.mult),
                      reads=[("sg", fc % 2)] + pcell(kB), writes=hT_cells(fc))
            gs = gb_load(1)
            for dh in range(2):
                for grp in range(6):
                    fcs = list(range(4 * grp, min(4 * grp + 4, NFC)))
                    nf = len(fcs)
                    src = wd_s[0].rearrange("(fc p) d -> p fc d", p=128)[:, fcs[0]:fcs[0] + nf, dh * 512:(dh + 1) * 512]
                    slot, scell = ring_load(src, lambda r, nf=nf: r[:, 0:nf * 512].rearrange("p (a d) -> p a d", d=512),
                                            [("wd_s", 0)])
                    for fi, fc in enumerate(fcs):
                        for b in range(4):
                            kk = 4 * dh + b
                            sc.op("pe", lambda e, fi=fi, fc=fc, b=b, kk=kk: e.matmul(
                                bank(kk), hT[:, fc, 128 * b:128 * (b + 1)], slot[:, fi, :],
                                start=(fc == 0), stop=(fc == NFC - 1)),
                                reads=scell + hT_cells(fc), writes=pcell(kk), signal=(fc == NFC - 1 or fi == nf - 1))
                for b in range(4):
                    kk = 4 * dh + b
                    sc.op("dve", lambda e, b=b, kk=kk: e.scalar_tensor_tensor(
                        xs2[:, par, b, dh * 512:(dh + 1) * 512], xs2[:, par, b, dh * 512:(dh + 1) * 512], ALPHA, bank(kk),
                        ALU.mult, ALU.add),
                        reads=xsc(par, b) + pcell(kk), writes=xsc(par, b))
            ln4(par, gs, 0)

        def ffn1_gen(ti):
            par = ti % 2
            xT_cells = [("xTB", b) for b in range(4)]
            pend = []
            for fc in range(NFC):
                slot, scell = ring_load(wgu_s[1][fc], lambda r: r.rearrange("p (a k f) -> p a k f", a=2, k=8),
                                        [wgu_cell(1, fc)])
                kA = 5 if fc % 2 == 0 else 7
                kB = 6
                for a, kk in ((0, kA), (1, kB)):
                    for half in range(2):
                        for kc in range(4 * half, 4 * half + 4):
                            sc.op("pe", lambda e, a=a, kk=kk, kc=kc: e.matmul(
                                bank(kk), slot[:, a, kc, :], xTB[:, kc, :], start=(kc == 0), stop=(kc == 7)),
                                reads=scell + xT_cells, writes=pcell(kk), signal=(kc == 7))
                        if pend:
                            pend.pop(0)()
                        yield

                def epi_act(kA=kA, fc=fc):
                    sc.op("act", lambda e: e.activation(sg[:, fc % 2, :], bank(kA), AF.Exp, scale=-1.0),
                          reads=pcell(kA), writes=[("sg", fc % 2)])

                def epi_dve(kA=kA, kB=kB, fc=fc):
                    sgc = [("sg", fc % 2)]
                    sc.op("dve", lambda e: e.tensor_scalar(sg[:, fc % 2, :], sg[:, fc % 2, :], 1.0, None, ALU.add),
                          reads=sgc, writes=sgc)
                    sc.op("dve", lambda e: e.reciprocal(sg[:, fc % 2, :], sg[:, fc % 2, :]),
                          reads=sgc, writes=sgc)
                    sc.op("dve", lambda e: e.tensor_tensor(sg[:, fc % 2, :], sg[:, fc % 2, :], bank(kB), ALU.mult),
                          reads=sgc + pcell(kB), writes=sgc)
                    sc.op("dve", lambda e: e.tensor_tensor(hT[:, fc, :], sg[:, fc % 2, :], bank(kA), ALU.mult),
                          reads=sgc + pcell(kA), writes=hT_cells(fc))
                pend = [epi_act, epi_dve]
            while pend:
                pend.pop(0)()
                yield
            gs = gb_load(3)
            for dh in range(2):
                for bp in range(2):
                    banks = (5, 7)
                    for grp in range(6):
                        fcs = list(range(4 * grp, min(4 * grp + 4, NFC)))
                        nf = len(fcs)
                        src = wd_s[1].rearrange("(fc p) d -> p fc d", p=128)[:, fcs[0]:fcs[0] + nf, dh * 512:(dh + 1) * 512]
                        slot, scell = ring_load(src, lambda r, nf=nf: r[:, 0:nf * 512].rearrange("p (a d) -> p a d", d=512),
                                                [("wd_s", 1)])
                        for fi, fc in enumerate(fcs):
                            for bi in range(2):
                                b = 2 * bp + bi
                                kk = banks[bi]
                                sc.op("pe", lambda e, fi=fi, fc=fc, b=b, kk=kk: e.matmul(
                                    bank(kk), hT[:, fc, 128 * b:128 * (b + 1)], slot[:, fi, :],
                                    start=(fc == 0), stop=(fc == NFC - 1)),
                                    reads=scell + hT_cells(fc), writes=pcell(kk),
                                    signal=(fc == NFC - 1 or fi == nf - 1))
                            if fi % 2 == 1 or fi == nf - 1:
                                yield
                    for bi in range(2):
                        b = 2 * bp + bi
                        kk = banks[bi]
                        sc.op("dve", lambda e, b=b, kk=kk: e.scalar_tensor_tensor(
                            xs2[:, par, b, dh * 512:(dh + 1) * 512], xs2[:, par, b, dh * 512:(dh + 1) * 512], ALPHA,
                            bank(kk), ALU.mult, ALU.add),
                            reads=xsc(par, b) + pcell(kk), writes=xsc(par, b))
                    yield
            yield from ln4_gen(par, gs, 1)
            ev = sc.dma("sp", "ost", lambda e: e.dma_start(
                out=out.rearrange("(b p) d -> p b d", p=128)[:, 4 * ti:4 * ti + 4, :], in_=xs2[:, par, :, :]),
                reads=[c for b in range(4) for c in xsc(par, b)])
            out_evs.append(ev)
            yield

        out_evs = []
        filler = None

        def fill(n=1):
            nonlocal filler
            for _ in range(n):
                if filler is None:
                    return
                try:
                    next(filler)
                except StopIteration:
                    filler = None
                    return

        def flush():
            while filler is not None:
                fill(1)

        gs0_of = {}

        def prefetch_x(ti):
            par = ti % 2
            r0 = T * ti
            sc.dma("sp", "xld", lambda e: e.dma_start(
                out=xs2[:, par, :, :], in_=x.rearrange("(b p) d -> p b d", p=128)[:, 4 * ti:4 * ti + 4, :]),
                writes=[c for b in range(4) for c in xsc(par, b)])
            if ti == 0:
                sc.op("dve", lambda e: e.memset(halo[:], 0.0), writes=["halo"])
            else:
                sc.dma("sp", "hld", lambda e: e.dma_start(out=halo[:], in_=x[r0 - 128:r0, :]),
                       writes=["halo"])
            gs0_of[ti] = gb_load(0)
            sc.op("act", lambda e: e.activation(xb[:, 0, :], halo[:], AF.Copy), reads=["halo"], writes=xb_cells(0))
            if ti == 0 and PREP:
                prep_wgu(0)
                prep_wd(0)
                prep_attn()
                prep_wgu(1)
                prep_wd(1)
            for b in range(4):
                if b < 2:
                    sc.op("act", lambda e, b=b: e.activation(xb[:, b + 1, :], xs2[:, par, b, :], AF.Copy),
                          reads=xsc(par, b), writes=xb_cells(b + 1))
                else:
                    sc.op("dve", lambda e, b=b: e.tensor_copy(xb[:, b + 1, :], xs2[:, par, b, :]),
                          reads=xsc(par, b), writes=xb_cells(b + 1))

        for i in range(NT):
            sc.epoch = i
            r0 = T * i
            par = i % 2
            if i == 0:
                prefetch_x(0)
            gs0 = gs0_of[i]
            for j in range(8):
                g = j // 2
                kk = j % 8
                for b in range(4):
                    cur = (8 + g) if (i == 0 and b == 0) else g
                    sc.op("pe", lambda e, j=j, b=b, cur=cur, kk=kk: e.matmul(
                        ps[:, kk, 128 * b:128 * (b + 1)], xb[:, b + 1, 128 * j:128 * (j + 1)], band[:, cur, :],
                        start=True, stop=False),
                        reads=xb_cells(b + 1) + ["band"], writes=pcell(kk), signal=False)
                    sc.op("pe", lambda e, j=j, b=b, g=g, kk=kk: e.matmul(
                        ps[:, kk, 128 * b:128 * (b + 1)], xb[:, b, 128 * j:128 * (j + 1)], band[:, 4 + g, :],
                        start=False, stop=True),
                        reads=xb_cells(b) + ["band"], writes=pcell(kk), signal=(b == 3))
                if j % 2 == 0:
                    sc.op("act", lambda e, j=j, kk=kk: e.activation(mo[:, j, :], bank(kk), AF.Copy),
                          reads=pcell(kk), writes=[("mo", j)])
                else:
                    sc.op("dve", lambda e, j=j, kk=kk: e.tensor_copy(mo[:, j, :], bank(kk)),
                          reads=pcell(kk), writes=[("mo", j)])
            for b in range(4):
                for g in range(4):
                    for k in range(2):
                        sc.op("pe", lambda e, b=b, g=g, k=k: e.matmul(
                            ps[:, 2 * b + g // 2, (g % 2) * 256:(g % 2) * 256 + 256],
                            mo[:, 2 * g + k, 128 * b:128 * (b + 1)], wp[:, 2 * g + k, :],
                            start=(k == 0), stop=(k == 1)),
                            reads=[("mo", 2 * g + k)] + wp_cells, writes=dpcell(b), signal=(g == 3 and k == 1))
                sc.op("dve", lambda e, b=b: e.scalar_tensor_tensor(
                    xs2[:, par, b, :], xs2[:, par, b, :], ALPHA, dbank(b), ALU.mult, ALU.add),
                    reads=xsc(par, b) + dpcell(b), writes=xsc(par, b))
            ln4(par, gs0, 0)
            for b in range(4):
                transpose_block(par, b, xTA, "xTA")
            ffn0(par)
            for b in range(4):
                transpose_block(par, b, xTA, "xTA")
            xT_cells = [("xTA", b) for b in range(4)]
            for op2 in range(8):
                slot, scell = ring_load(wqk_s[2 * op2:2 * op2 + 2].rearrange("o p k f -> p o k f"),
                                        lambda r: r.rearrange("p (a k f) -> p a k f", a=2, k=8), ["wqk_s"])
                for a in range(2):
                    oc = 2 * op2 + a
                    kk = oc % 8
                    for kc in range(8):
                        sc.op("pe", lambda e, a=a, kk=kk, kc=kc: e.matmul(
                            bank(kk), slot[:, a, kc, :], xTA[:, kc, :], start=(kc == 0), stop=(kc == 7)),
                            reads=scell + xT_cells, writes=pcell(kk), signal=(kc == 7))
                    dst = qkt[:, oc, :] if oc < 8 else kst[:, oc - 8, :]
                    dcell = [("qkt", oc)] if oc < 8 else [kst_cell[oc - 8]]
                    if oc % 2 == 0:
                        sc.op("act", lambda e, kk=kk: e.activation(dst, bank(kk), AF.Copy),
                              reads=pcell(kk), writes=dcell)
                    else:
                        sc.op("dve", lambda e, kk=kk: e.tensor_copy(dst, bank(kk)),
                              reads=pcell(kk), writes=dcell)
            for dh in range(2):
                for grp in range(2):
                    src = wv_s.rearrange("(kc p) d -> p kc d", p=128)[:, 4 * grp:4 * grp + 4, dh * 512:(dh + 1) * 512]
                    slot, scell = ring_load(src, lambda r: r.rearrange("p (a d) -> p a d", d=512), ["wv_s"])
                    for ki in range(4):
                        kc = 4 * grp + ki
                        for b in range(4):
                            kk = 4 * dh + b
                            sc.op("pe", lambda e, ki=ki, kc=kc, b=b, kk=kk: e.matmul(
                                bank(kk), xTA[:, kc, 128 * b:128 * (b + 1)], slot[:, ki, :],
                                start=(kc == 0), stop=(kc == 7)),
                                reads=scell + [("xTA", b)], writes=pcell(kk), signal=(kc == 7 or ki == 3))
                for b in range(4):
                    kk = 4 * dh + b
                    if b % 2 == 0:
                        sc.op("act", lambda e, b=b, kk=kk, dh=dh: e.activation(
                            vt[:, b, dh * 512:(dh + 1) * 512], bank(kk), AF.Copy),
                            reads=pcell(kk), writes=[vt_cell[b]])
                    else:
                        sc.op("dve", lambda e, b=b, kk=kk, dh=dh: e.tensor_copy(
                            vt[:, b, dh * 512:(dh + 1) * 512], bank(kk)),
                            reads=pcell(kk), writes=[vt_cell[b]])
            sc.dma("sp", "kw", lambda e: e.dma_start(
                out=kt_s.rearrange("h p s -> p h s")[:, :, r0:r0 + T], in_=kst),
                reads=[("P", 0), ("P", 1), ("P", 2), ("a", 0)], writes=["kt_s"])
            sc.dma("sp", "vw", lambda e: e.dma_start(
                out=v_s.rearrange("(b p) d -> p b d", p=128)[:, 4 * i:4 * i + 4, :], in_=vt[:, :, :]),
                reads=vt_cell, writes=["v_s"])

            nkb = 4 * i + 4
            nkeys = 128 * nkb
            steps = [(hp, k) for hp in range(8) for k in range(nkb)]
            nst = len(steps)

            def kv_load(hp):
                pb = hp % 2
                sc.dma("sp", f"kld{pb}", lambda e: e.dma_start(
                    out=kbuf[pb][:, 0:nkeys], in_=kt_s[hp][:, 0:nkeys]),
                    reads=["kt_s"], writes=kbuf_cells[pb])
                sc.dma("sp", f"vld{pb}", lambda e: e.dma_start(
                    out=vbuf[pb][:, 0:nkb, :],
                    in_=v_s.rearrange("(k p) d -> p k d", p=128)[:, 0:nkb, 128 * hp:128 * (hp + 1)]),
                    reads=["v_s"], writes=vbuf_cells[pb])

            def c0_of(m):
                hp, k = steps[m]
                kb = nkb - 1 - k
                return 128 * (kb - 4 * i) if kb >= 4 * i else 0

            def v3(ap2, c0):
                return ap2.rearrange("p (l n) -> p l n", l=2)[:, :, c0:512]

            def st_Z(m):
                hp, k = steps[m]
                kb = nkb - 1 - k
                diag = kb >= 4 * i
                jj = kb - 4 * i
                c0 = c0_of(m)
                for l in range(2):
                    sc.op("pe", lambda e, l=l: e.matmul(
                        ps[:, l, c0:512], kbuf[hp % 2][64 * l:64 * l + 64, 128 * kb:128 * (kb + 1)],
                        qkt[64 * l:64 * l + 64, hp, c0:512], start=True, stop=(not diag)),
                        reads=kbuf_cells[hp % 2] + [("qkt", hp)], writes=dpcell(0),
                        signal=(l == 1 and not diag))
                if diag:
                    for l in range(2):
                        sc.op("pe", lambda e, l=l: e.matmul(
                            ps[:, l, c0:512], tri[:, 2, :], mask[:, jj, c0:512], start=False, stop=True),
                            reads=["tri", "mask"], writes=dpcell(0), signal=(l == 1))

            def st_EP(m):
                c0 = c0_of(m)
                sc.op("act", lambda e: e.activation(v3(Eb[:, m % 3, :], c0), ps[:, 0:2, c0:512], AF.Exp, scale=0.125),
                      reads=dpcell(0), writes=[("E", m % 3)])
                sc.op("act", lambda e: e.activation(v3(Pb[:, m % 3, :], c0), v3(Eb[:, m % 3, :], c0), AF.Ln, bias=1.0),
                      reads=[("E", m % 3)], writes=[("P", m % 3)])

            def st_C(m):
                hp, k = steps[m]
                c0 = c0_of(m)
                for l in range(2):
                    if k == 0:
                        sc.op("pe", lambda e, l=l: e.matmul(
                            ps[:, 2 + l, :], tri[:, 3, :], mask[:, 0, :], start=True, stop=True,
                            skip_group_check=True),
                            reads=["tri", "mask"], writes=dpcell(1), signal=False)
                    else:
                        cp = c0_of(m - 1)
                        sc.op("pe", lambda e, l=l: e.matmul(
                            ps[:, 2 + l, cp:512], tri[:, 1, :], Pb[:, (m - 1) % 3, 512 * l + cp:512 * (l + 1)],
                            start=False, stop=False, skip_group_check=True),
                            reads=[("P", (m - 1) % 3), "tri"], writes=dpcell(1), signal=False)
                    sc.op("pe", lambda e, l=l: e.matmul(
                        ps[:, 2 + l, c0:512], tri[:, 0, :], Pb[:, m % 3, 512 * l + c0:512 * (l + 1)],
                        start=False, stop=True, skip_group_check=True),
                        reads=[("P", m % 3), "tri"], writes=dpcell(1), signal=(l == 1))

            def st_G(m):
                c0 = c0_of(m)
                sc.op("act", lambda e: e.activation(v3(Gb[:, m % 2, :], c0), ps[:, 2:4, c0:512], AF.Exp, scale=-1.0),
                      reads=dpcell(1), writes=[("G", m % 2)])

            def st_A(m):
                c0 = c0_of(m)
                sc.op("dve", lambda e: e.tensor_tensor(v3(ab[:, m % 2, :], c0), v3(Eb[:, m % 3, :], c0),
                                                       v3(Gb[:, m % 2, :], c0), ALU.mult),
                      reads=[("E", m % 3), ("G", m % 2)], writes=[("a", m % 2)])

            def st_O(m):
                hp, k = steps[m]
                kb = nkb - 1 - k
                c0 = c0_of(m)
                for l in range(2):
                    if k == 0:
                        sc.op("pe", lambda e, l=l: e.matmul(
                            ps[64 * l:64 * l + 64, 4, :], tri[:, 3, 0:64], mask[:, 0, :], start=True, stop=False,
                            skip_group_check=True),
                            reads=["tri", "mask"], writes=pcell(4), signal=False)
                    sc.op("pe", lambda e, l=l: e.matmul(
                        ps[64 * l:64 * l + 64, 4, c0:512], vbuf[hp % 2][:, kb, 64 * l:64 * l + 64],
                        ab[:, m % 2, 512 * l + c0:512 * (l + 1)], start=False, stop=(k == nkb - 1),
                        skip_group_check=True),
                        reads=vbuf_cells[hp % 2] + [("a", m % 2)], writes=pcell(4), signal=(l == 1))
                if k == nkb - 1:
                    sc.op("dve", lambda e: e.tensor_copy(mo[:, hp, :], bank(4)),
                          reads=pcell(4), writes=[("mo", hp)])

            frate = 0.0 if i == 1 else min(2.0, FR / nst)
            facc = 0.0
            kv_load(0)
            st_Z(0)
            for m in range(nst + 2):
                if m < nst:
                    hp, k = steps[m]
                    if k == 2 and hp + 1 < 8:
                        kv_load(hp + 1)
                    st_EP(m)
                if 0 <= m - 1 < nst:
                    st_C(m - 1)
                    st_G(m - 1)
                    st_A(m - 1)
                if m + 1 < nst:
                    st_Z(m + 1)
                    for _ in range(DUMMY):
                        st_Z(m + 1)
                if 0 <= m - 2 < nst:
                    st_O(m - 2)
                facc += frate
                while facc >= 1.0:
                    fill(1)
                    facc -= 1.0

            gs2 = gb_load(2)
            lns = []

            def advance_lns():
                for g in lns:
                    next(g, None)

            for b in range(4):
                for dh in range(2):
                    kk = (2 * b + dh) % 5
                    for kc in range(8):
                        sc.op("pe", lambda e, kc=kc, b=b, dh=dh, kk=kk: e.matmul(
                            bank(kk), mo[:, kc, 128 * b:128 * (b + 1)], wo_sb[:, kc, dh * 512:(dh + 1) * 512],
                            start=(kc == 0), stop=(kc == 7)),
                            reads=["wo_sb", ("mo", kc)], writes=pcell(kk), signal=(kc == 7))
                    sc.op("dve", lambda e, b=b, dh=dh, kk=kk: e.scalar_tensor_tensor(
                        xs2[:, par, b, dh * 512:(dh + 1) * 512], xs2[:, par, b, dh * 512:(dh + 1) * 512], ALPHA, bank(kk),
                        ALU.mult, ALU.add),
                        reads=xsc(par, b) + pcell(kk), writes=xsc(par, b))
                    advance_lns()
                lns.append(ln_gen(par, b, gs2, 0))
                fill(1)
            for _ in range(6):
                advance_lns()
                fill(1)
            flush()
            if i + 1 < NT:
                prefetch_x(i + 1)
            for b in range(4):
                transpose_block(par, b, xTB, "xTB")
            filler = ffn1_gen(i)
        flush()
        sc.wait_only("sp", out_evs)

        sems = {}
        for key in sorted(sc.keys, key=str):
            sems[key] = es.enter_context(nc.semaphore("s_" + "_".join(str(t) for t in key)))
        sc.replay(nc, sems)
    return nc


def make_consts():
    bf = ml_dtypes.bfloat16
    ident = np.eye(128, dtype=np.float32)
    j = np.arange(128)[:, None]
    s = np.arange(128)[None, :]
    tri = np.zeros((128, 4, 128), np.float32)
    tri[:, 0, :] = (j >= s)
    tri[:, 1, :] = (j < s)
    tri[:, 2, :] = -2048.0 * (j == s)
    mask = np.zeros((128, 4, 512), np.float32)
    t = np.arange(512)[None, :]
    for jj in range(4):
        mask[:, jj, :] = ((128 * jj + np.arange(128)[:, None]) >= t)
    band = np.zeros((128, 12, 128), np.float32)
    tp = np.arange(128)[:, None]
    tt = np.arange(128)[None, :]
    for g, w in enumerate(WINDOWS):
        cur = ((tp <= tt) & (tp > tt - w)).astype(np.float32) / w
        band[:, g, :] = cur - (tp == tt)
        band[:, 4 + g, :] = ((tp - 128) > (tt - w)).astype(np.float32) / w
        cnt = np.minimum(tt + 1, w).astype(np.float32)
        band[:, 8 + g, :] = ((tp <= tt) & (tp > tt - w)).astype(np.float32) / cnt - (tp == tt)
    return {"c_ident": ident, "c_tri": tri.astype(bf), "c_mask": mask.astype(bf), "c_band": band.astype(bf)}


def make_in_maps(x, ln_mix_g, ln_mix_b, ln_ffn_g, ln_ffn_b, pool_w, pool_scale, w_qkv, w_o, w_gate, w_up, w_down):
    f = lambda a: np.ascontiguousarray(np.asarray(a, dtype=np.float32))
    gb = np.stack([f(ln_mix_g)[0], f(ln_mix_b)[0], f(ln_ffn_g)[0], f(ln_ffn_b)[0],
                   f(ln_mix_g)[1], f(ln_mix_b)[1], f(ln_ffn_g)[1], f(ln_ffn_b)[1]], axis=0).reshape(-1)
    shared = {
        "gb": np.ascontiguousarray(gb), "pool_w": f(pool_w)[0], "pool_scale": f(pool_scale)[0],
        "w_qkv": f(w_qkv)[0], "w_o": f(w_o)[0], "w_gu": np.ascontiguousarray(np.stack([f(w_gate), f(w_up)], axis=1)),
        "w_down": f(w_down),
    }
    shared.update(make_consts())
    x = f(x)
    return [dict(shared, x=x[c]) for c in range(8)]


def kernel(x, ln_mix_g, ln_mix_b, ln_ffn_g, ln_ffn_b, pool_w, pool_scale, w_qkv, w_o, w_gate, w_up, w_down):
    in_maps = make_in_maps(x, ln_mix_g, ln_mix_b, ln_ffn_g, ln_ffn_b, pool_w, pool_scale,
                           w_qkv, w_o, w_gate, w_up, w_down)
    nc = build(NTILES)
    res = run_bass_kernel_spmd(nc, in_maps, core_ids=list(range(8)))
    return np.stack([np.asarray(r["out"]).astype(np.float32) for r in res.results], axis=0)
```

```python
import math
import types
from contextlib import ExitStack

import numpy as np
import ml_dtypes
import concourse.bass as bass
import concourse.mybir as mybir
from concourse.bass_utils import run_bass_kernel_spmd

F32 = mybir.dt.float32
BF16 = mybir.dt.bfloat16
AF = mybir.ActivationFunctionType
ALU = mybir.AluOpType

D = 1024
FF = 2816
NFC = FF // 128
S = 4096
T = 512
NTILES = S // T
ALPHA = float((2.0 * 2) ** 0.25)
EPS = 1e-5
NSLOT = 4
WINDOWS = (2, 4, 8, 16)


def _snap(fn):
    if fn is None or fn.__closure__ is None:
        return fn
    cells = []
    for c in fn.__closure__:
        try:
            cells.append(types.CellType(c.cell_contents))
        except ValueError:
            cells.append(c)
    return types.FunctionType(fn.__code__, fn.__globals__, fn.__name__, fn.__defaults__, tuple(cells))


class Sched:
    ENGS = ("pe", "act", "dve", "pool", "sp")

    def __init__(self):
        self.q = {e: [] for e in self.ENGS}
        self.cnt = {}
        self.epoch = 0
        self.lastw = {}
        self.readers = {}
        self.dmacnt = {}
        self.keys = set()
        self.pending = {e: ([], []) for e in self.ENGS}

    def _deps(self, reads, writes):
        waits = set()
        for c in reads:
            if c in self.lastw:
                waits.add(self.lastw[c])
        for c in writes:
            if c in self.lastw:
                waits.add(self.lastw[c])
            for r in self.readers.get(c, ()):
                waits.add(r)
        return waits

    def _commit(self, ev, reads, writes):
        for c in reads:
            self.readers.setdefault(c, []).append(ev)
        for c in writes:
            self.lastw[c] = ev
            self.readers[c] = []

    def op(self, eng, fn, reads=(), writes=(), signal=True):
        fn = _snap(fn)
        waits = self._deps(reads, writes)
        pr, pw = self.pending[eng]
        pr.extend(reads)
        pw.extend(writes)
        if signal:
            key = (eng, self.epoch)
            self.cnt[key] = self.cnt.get(key, 0) + 1
            ev = (key, self.cnt[key])
            self.keys.add(key)
            self.q[eng].append((waits, fn, (key, 1)))
            self._commit(ev, pr, pw)
            self.pending[eng] = ([], [])
            return ev
        self.q[eng].append((waits, fn, None))
        return None

    def dma(self, eng, sem, fn, reads=(), writes=()):
        fn = _snap(fn)
        waits = self._deps(reads, writes)
        key = ("dma", sem)
        self.dmacnt[key] = self.dmacnt.get(key, 0) + 16
        ev = (key, self.dmacnt[key])
        self.keys.add(key)
        self.q[eng].append((waits, fn, (key, 16)))
        self._commit(ev, reads, writes)
        return ev

    def wait_only(self, eng, evs):
        self.q[eng].append((set(evs), None, None))

    def replay(self, nc, sems):
        handles = {"pe": "tensor", "act": "scalar", "dve": "vector", "pool": "gpsimd", "sp": "sync"}
        with nc.Block() as block:
            for eng in self.ENGS:
                def body(e, eng=eng):
                    waited = {}
                    for waits, fn, sig in self.q[eng]:
                        best = {}
                        for k, v in waits:
                            if eng == "pe" and k[0] == "pe":
                                continue
                            if best.get(k, 0) < v:
                                best[k] = v
                        for k, v in sorted(best.items(), key=lambda kv: str(kv[0])):
                            if waited.get(k, 0) < v:
                                e.wait_ge(sems[k], v)
                                waited[k] = v
                        if fn is None:
                            continue
                        ins = fn(e)
                        if sig is not None:
                            ins.then_inc(sems[sig[0]], sig[1])
                getattr(block, handles[eng])(body)


def build(NT=NTILES, LIM=99, PREP=True, ALIM=99, FILL=4, DUMMY=0, FR=140.0):
    nc = bass.Bass("TRN2", target_bir_lowering=False)
    sc = Sched()

    def din(name, shape, dt=F32):
        return nc.dram_tensor(name, list(shape), dt, kind="ExternalInput").ap()

    def dscr(name, shape, dt=BF16):
        return nc.dram_tensor(name, list(shape), dt, kind="Internal").ap()

    x = din("x", [S, D])
    gb = din("gb", [8 * D])
    pool_w = din("pool_w", [4, 256, 256])
    pool_scale = din("pool_scale", [D])
    w_qkv = din("w_qkv", [D, 3 * D])
    w_o = din("w_o", [D, D])
    w_gu = din("w_gu", [2, 2, D, FF])
    w_down = din("w_down", [2, FF, D])
    c_ident = din("c_ident", [128, 128])
    c_tri = din("c_tri", [128, 4, 128], BF16)
    c_mask = din("c_mask", [128, 4, 512], BF16)
    c_band = din("c_band", [128, 12, 128], BF16)
    out = nc.dram_tensor("out", [S, D], F32, kind="ExternalOutput").ap()

    wgu_s = [dscr(f"wgu_s{l}", [NFC, 128, 2, 8, 128]) for l in range(2)]
    wd_s = [dscr(f"wd_s{l}", [FF, D]) for l in range(2)]
    wqk_s = dscr("wqk_s", [16, 128, 8, 128])
    wv_s = dscr("wv_s", [D, D])
    wo_s = dscr("wo_s", [D, D])
    kt_s = dscr("kt_s", [8, 128, S])
    v_s = dscr("v_s", [S, D])

    es = ExitStack()
    with es:
        def sb(name, shape, dt):
            return es.enter_context(nc.sbuf_tensor(name, list(shape), dt))

        xs2 = sb("xs2", [128, 2, 4, D], F32)
        halo = sb("halo", [128, D], F32)
        xTA = sb("xTA", [128, 8, T], BF16)
        xTB = sb("xTB", [128, 8, T], BF16)
        mo = sb("mo", [128, 8, T], BF16)
        R = sb("R", [128, NFC * T], BF16)
        kvb = sb("kvb", [128, 16384], BF16)
        qkt = sb("qkt", [128, 8, T], BF16)
        ring = sb("ring", [128, NSLOT, 2048], BF16)
        wo_sb = sb("wo_sb", [128, 8, D], BF16)
        gbuf = sb("gbuf", [128, 2, 2 * D], F32)
        ident = sb("ident", [128, 128], F32)
        tri = sb("tri", [128, 4, 128], BF16)
        mask = sb("mask", [128, 4, 512], BF16)
        band = sb("band", [128, 12, 128], BF16)
        wp = sb("wp", [128, 8, 256], BF16)
        sg = sb("sg", [128, 2, T], F32)
        lnst = sb("lnst", [128, 8, 12], F32)
        lnmv = sb("lnmv", [128, 8, 2], F32)
        lnt = sb("lnt", [128, 8, 4], F32)
        EG = sb("EG", [128, 5, 1024], BF16)
        Eb = EG[:, 0:3, :]
        Gb = EG[:, 3:5, :]
        vt = EG[:, 0:4, :]
        vt_cell = [("E", 0), ("E", 1), ("E", 2), ("G", 0)]
        KP = sb("KP", [128, 5, 1024], BF16)
        Pb = KP[:, 0:3, :]
        ab = KP[:, 3:5, :]
        kst = KP[:, 0:4, :].rearrange("p a (c t) -> p (a c) t", t=T)
        kst_cell = [("P", 0), ("P", 0), ("P", 1), ("P", 1), ("P", 2), ("P", 2), ("a", 0), ("a", 0)]
        ps = es.enter_context(nc.psum_tensor("ps", [128, 8, 512], F32))

        hT = R[:, 0:NFC * T].rearrange("p (c t) -> p c t", t=T)
        xb = R[:, 0:5 * D].rearrange("p (b d) -> p b d", d=D)
        kbuf = [kvb[:, 0:4096], kvb[:, 4096:8192]]
        vbuf = [kvb[:, 8192:12288].rearrange("p (k d) -> p k d", d=128),
                kvb[:, 12288:16384].rearrange("p (k d) -> p k d", d=128)]

        def hT_cells(fc):
            return [("R", fc)]

        def xb_cells(b):
            return [("R", 2 * b), ("R", 2 * b + 1)]

        kbuf_cells = [[("kb", 0)], [("kb", 1)]]
        vbuf_cells = [[("vb", 0)], [("vb", 1)]]

        def xsc(par, b):
            return [("xs", par, b)]

        sc.dma("sp", "c0", lambda e: e.dma_start(out=ident[:], in_=c_ident), writes=["ident"])
        sc.dma("sp", "c1", lambda e: e.dma_start(out=tri[:], in_=c_tri), writes=["tri"])
        sc.dma("sp", "c2", lambda e: e.dma_start(out=mask[:], in_=c_mask), writes=["mask"])
        sc.dma("sp", "c3", lambda e: e.dma_start(out=band[:], in_=c_band), writes=["band"])
        wpf = xs2[:, 1, 0:2, :].rearrange("p a (k d) -> p (a k) d", d=256)
        sc.dma("sp", "c5", lambda e: e.dma_start(
            out=wpf, in_=pool_w.rearrange("g (k p) d -> p (g k) d", p=128)),
            writes=xsc(1, 0) + xsc(1, 1))
        sc.dma("sp", "c6", lambda e: e.dma_start(out=halo[:], in_=pool_scale.partition_broadcast(128)),
               writes=["halo"])
        for g in range(4):
            for k in range(2):
                sc.op("dve", lambda e, g=g, k=k: e.tensor_tensor(
                    wp[:, 2 * g + k, :], wpf[:, 2 * g + k, :], halo[:, 256 * g:256 * (g + 1)], ALU.mult),
                    reads=xsc(1, 0) + xsc(1, 1) + ["halo"], writes=[("wp", 2 * g + k)])
        wp_cells = [("wp", j) for j in range(8)]

        def prep_wgu(l):
            src = w_gu[l].rearrange("a (kc p) (fc f) -> fc p (a kc) f", p=128, f=128)
            for fc in range(NFC):
                grp = fc if l == 0 else (0 if fc < 4 else (1 if fc < 12 else 2))
                cell = ("wgu_s", l, grp)
                sc.dma("pool", f"pwgu{l}_{grp}", lambda e, fc=fc: e.dma_start(
                    out=wgu_s[l][fc].rearrange("p a k f -> p (a k) f"), in_=src[fc]), writes=[cell])

        def wgu_cell(l, fc):
            return ("wgu_s", l, fc if l == 0 else (0 if fc < 4 else (1 if fc < 12 else 2)))

        def prep_wd(l):
            for h in range(2):
                sc.dma("pool", f"pwd{l}", lambda e, h=h: e.dma_start(
                    out=wd_s[l][h * 1408:(h + 1) * 1408, :], in_=w_down[l][h * 1408:(h + 1) * 1408, :]),
                    writes=[("wd_s", l)])

        def prep_attn():
            for oc in range(16):
                sc.dma("pool", "pwqk", lambda e, oc=oc: e.dma_start(
                    out=wqk_s[oc], in_=w_qkv[:, oc * 128:(oc + 1) * 128].rearrange("(kc p) f -> p kc f", p=128)),
                    writes=["wqk_s"])
            sc.dma("pool", "pwv", lambda e: e.dma_start(out=wv_s, in_=w_qkv[:, 2048:3072]), writes=["wv_s"])
            sc.dma("pool", "pwo", lambda e: e.dma_start(
                out=wo_sb[:, :, :], in_=w_o.rearrange("(kc p) d -> p kc d", p=128)), writes=["wo_sb"])

        ring_state = {"n": 0}

        def ring_load(src_ap, view_fn, src_cells):
            s = ring_state["n"] % NSLOT
            ring_state["n"] += 1
            dst = view_fn(ring[:, s, :])
            cell = ("ring", s)
            sc.dma("sp", f"ring{s}", lambda e: e.dma_start(out=dst, in_=src_ap),
                   reads=src_cells, writes=[cell])
            return dst, [cell]

        gb_state = {"n": 0}

        def gb_load(kidx):
            s = gb_state["n"] % 2
            gb_state["n"] += 1
            sc.dma("sp", f"gb{s}", lambda e: e.dma_start(
                out=gbuf[:, s, :], in_=gb[2 * D * kidx:2 * D * (kidx + 1)].partition_broadcast(128)),
                writes=[("gb", s)])
            return s

        def bank(k):
            return ps[:, k, :]

        def dbank(k):
            return ps[:, 2 * k:2 * k + 2, :].rearrange("p a n -> p (a n)")

        def pcell(k):
            return [("ps", k)]

        def dpcell(k):
            return [("ps", 2 * k), ("ps", 2 * k + 1)]

        def ln_gen(par, b, gs, st):
            xc = xsc(par, b)
            q = 4 * st + b
            xrow = xs2[:, par, b, :]
            sc.op("dve", lambda e: e.bn_stats(lnst[:, q, 0:6], xs2[:, par, b, 0:512]), reads=xc, writes=[("ln", q)])
            sc.op("dve", lambda e: e.bn_stats(lnst[:, q, 6:12], xs2[:, par, b, 512:1024]), reads=xc, writes=[("ln2", q)])
            sc.op("dve", lambda e: e.bn_aggr(lnmv[:, q, :], lnst[:, q, :].rearrange("p (j t) -> p j t", t=6)),
                  reads=[("ln", q), ("ln2", q)], writes=[("lnmv", q)])
            sc.op("dve", lambda e: e.tensor_scalar(lnt[:, q, 0:1], lnmv[:, q, 1:2], EPS, None, ALU.add),
                  reads=[("lnmv", q)], writes=[("lnt0", q)])
            yield
            sc.op("act", lambda e: e.activation(lnt[:, q, 1:2], lnt[:, q, 0:1], AF.Ln),
                  reads=[("lnt0", q)], writes=[("lnt1", q)])
            sc.op("act", lambda e: e.activation(lnt[:, q, 2:3], lnt[:, q, 1:2], AF.Exp, scale=-0.5),
                  reads=[("lnt1", q)], writes=[("lnt2", q)])
            yield
            sc.op("dve", lambda e: e.tensor_scalar(lnt[:, q, 3:4], lnmv[:, q, 0:1], lnt[:, q, 2:3], -1.0,
                                                   ALU.mult, ALU.mult),
                  reads=[("lnmv", q), ("lnt2", q)], writes=[("lnt3", q)])
            yield
            sc.op("act", lambda e: e.activation(xrow, xrow, AF.Identity, bias=lnt[:, q, 3:4], scale=lnt[:, q, 2:3]),
                  reads=xc + [("lnt2", q), ("lnt3", q)], writes=xc)
            yield
            sc.op("dve", lambda e: e.tensor_tensor(xrow, xrow, gbuf[:, gs, 0:D], ALU.mult),
                  reads=xc + [("gb", gs)], writes=xc)
            yield
            beng = "dve"
            sc.op(beng, lambda e: e.tensor_tensor(xrow, xrow, gbuf[:, gs, D:2 * D], ALU.add),
                  reads=xc + [("gb", gs)], writes=xc)
            yield

        def ln4_gen(par, gs, st):
            gens = [ln_gen(par, b, gs, st) for b in range(4)]
            for _stage in range(6):
                for g in gens:
                    next(g, None)
                yield

        def ln4(par, gs, st):
            for _ in ln4_gen(par, gs, st):
                pass

        def transpose_block(par, b, xT, tag):
            k = b
            tp = dbank(k).rearrange("p (j t) -> p j t", t=128)
            for j in range(8):
                sc.op("pe", lambda e, j=j: e.transpose(tp[:, j, :], xs2[:, par, b, 128 * j:128 * (j + 1)], ident[:]),
                      reads=xsc(par, b) + ["ident"], writes=dpcell(k), signal=(j == 7))
            if b % 2 == 0:
                sc.op("act", lambda e: e.activation(xT[:, :, 128 * b:128 * (b + 1)], tp, AF.Copy),
                      reads=dpcell(k), writes=[(tag, b)])
            else:
                sc.op("dve", lambda e: e.tensor_copy(xT[:, :, 128 * b:128 * (b + 1)], tp),
                      reads=dpcell(k), writes=[(tag, b)])

        def ffn0(par):
            xT_cells = [("xTA", b) for b in range(4)]
            for fc in range(NFC):
                slot, scell = ring_load(wgu_s[0][fc], lambda r: r.rearrange("p (a k f) -> p a k f", a=2, k=8),
                                        [wgu_cell(0, fc)])
                kA, kB = 2 * (fc % 4), 2 * (fc % 4) + 1
                for a, kk in ((0, kA), (1, kB)):
                    for kc in range(8):
                        sc.op("pe", lambda e, a=a, kk=kk, kc=kc: e.matmul(
                            bank(kk), slot[:, a, kc, :], xTA[:, kc, :], start=(kc == 0), stop=(kc == 7)),
                            reads=scell + xT_cells, writes=pcell(kk), signal=(kc == 7))
                sc.op("act", lambda e, kA=kA, fc=fc: e.activation(sg[:, fc % 2, :], bank(kA), AF.Silu),
                      reads=pcell(kA), writes=[("sg", fc % 2)])
                sc.op("dve", lambda e, kB=kB, fc=fc: e.tensor_tensor(hT[:, fc, :], sg[:, fc % 2, :], bank(kB), ALU.mult),
                      reads=[("sg", fc % 2)] + pcell(kB), writes=hT_cells(fc))
            gs = gb_load(1)
            for dh in range(2):
                for grp in range(6):
                    fcs = list(range(4 * grp, min(4 * grp + 4, NFC)))
                    nf = len(fcs)
                    src = wd_s[0].rearrange("(fc p) d -> p fc d", p=128)[:, fcs[0]:fcs[0] + nf, dh * 512:(dh + 1) * 512]
                    slot, scell = ring_load(src, lambda r, nf=nf: r[:, 0:nf * 512].rearrange("p (a d) -> p a d", d=512),
                                            [("wd_s", 0)])
                    for fi, fc in enumerate(fcs):
                        for b in range(4):
                            kk = 4 * dh + b
                            sc.op("pe", lambda e, fi=fi, fc=fc, b=b, kk=kk: e.matmul(
                                bank(kk), hT[:, fc, 128 * b:128 * (b + 1)], slot[:, fi, :],
                                start=(fc == 0), stop=(fc == NFC - 1)),
                                reads=scell + hT_cells(fc), writes=pcell(kk), signal=(fc == NFC - 1 or fi == nf - 1))
                for b in range(4):
                    kk = 4 * dh + b
                    sc.op("dve", lambda e, b=b, kk=kk: e.scalar_tensor_tensor(
                        xs2[:, par, b, dh * 512:(dh + 1) * 512], xs2[:, par, b, dh * 512:(dh + 1) * 512], ALPHA, bank(kk),
                        ALU.mult, ALU.add),
                        reads=xsc(par, b) + pcell(kk), writes=xsc(par, b))
            ln4(par, gs, 0)

        def ffn1_gen(ti):
            par = ti % 2
            xT_cells = [("xTB", b) for b in range(4)]
            pend = []
            for fc in range(NFC):
                slot, scell = ring_load(wgu_s[1][fc], lambda r: r.rearrange("p (a k f) -> p a k f", a=2, k=8),
                                        [wgu_cell(1, fc)])
                kA = 5 if fc % 2 == 0 else 7
                kB = 6
                for a, kk in ((0, kA), (1, kB)):
                    for half in range(2):
                        for kc in range(4 * half, 4 * half + 4):
                            sc.op("pe", lambda e, a=a, kk=kk, kc=kc: e.matmul(
                                bank(kk), slot[:, a, kc, :], xTB[:, kc, :], start=(kc == 0), stop=(kc == 7)),
                                reads=scell + xT_cells, writes=pcell(kk), signal=(kc == 7))
                        if pend:
                            pend.pop(0)()
                        yield

                def epi_act(kA=kA, fc=fc):
                    sc.op("act", lambda e: e.activation(sg[:, fc % 2, :], bank(kA), AF.Exp, scale=-1.0),
                          reads=pcell(kA), writes=[("sg", fc % 2)])

                def epi_dve(kA=kA, kB=kB, fc=fc):
                    sgc = [("sg", fc % 2)]
                    sc.op("dve", lambda e: e.tensor_scalar(sg[:, fc % 2, :], sg[:, fc % 2, :], 1.0, None, ALU.add),
                          reads=sgc, writes=sgc)
                    sc.op("dve", lambda e: e.reciprocal(sg[:, fc % 2, :], sg[:, fc % 2, :]),
                          reads=sgc, writes=sgc)
                    sc.op("dve", lambda e: e.tensor_tensor(sg[:, fc % 2, :], sg[:, fc % 2, :], bank(kB), ALU.mult),
                          reads=sgc + pcell(kB), writes=sgc)
                    sc.op("dve", lambda e: e.tensor_tensor(hT[:, fc, :], sg[:, fc % 2, :], bank(kA), ALU.mult),
                          reads=sgc + pcell(kA), writes=hT_cells(fc))
                pend = [epi_act, epi_dve]
            while pend:
                pend.pop(0)()
                yield
            gs = gb_load(3)
            for dh in range(2):
                for bp in range(2):
                    banks = (5, 7)
                    for grp in range(6):
                        fcs = list(range(4 * grp, min(4 * grp + 4, NFC)))
                        nf = len(fcs)
                        src = wd_s[1].rearrange("(fc p) d -> p fc d", p=128)[:, fcs[0]:fcs[0] + nf, dh * 512:(dh + 1) * 512]
                        slot, scell = ring_load(src, lambda r, nf=nf: r[:, 0:nf * 512].rearrange("p (a d) -> p a d", d=512),
                                                [("wd_s", 1)])
                        for fi, fc in enumerate(fcs):
                            for bi in range(2):
                                b = 2 * bp + bi
                                kk = banks[bi]
                                sc.op("pe", lambda e, fi=fi, fc=fc, b=b, kk=kk: e.matmul(
                                    bank(kk), hT[:, fc, 128 * b:128 * (b + 1)], slot[:, fi, :],
                                    start=(fc == 0), stop=(fc == NFC - 1)),
                                    reads=scell + hT_cells(fc), writes=pcell(kk),
                                    signal=(fc == NFC - 1 or fi == nf - 1))
                            if fi % 2 == 1 or fi == nf - 1:
                                yield
                    for bi in range(2):
                        b = 2 * bp + bi
                        kk = banks[bi]
                        sc.op("dve", lambda e, b=b, kk=kk: e.scalar_tensor_tensor(
                            xs2[:, par, b, dh * 512:(dh + 1) * 512], xs2[:, par, b, dh * 512:(dh + 1) * 512], ALPHA,
                            bank(kk), ALU.mult, ALU.add),
                            reads=xsc(par, b) + pcell(kk), writes=xsc(par, b))
                    yield
            yield from ln4_gen(par, gs, 1)
            ev = sc.dma("sp", "ost", lambda e: e.dma_start(
                out=out.rearrange("(b p) d -> p b d", p=128)[:, 4 * ti:4 * ti + 4, :], in_=xs2[:, par, :, :]),
                reads=[c for b in range(4) for c in xsc(par, b)])
            out_evs.append(ev)
            yield

        out_evs = []
        filler = None

        def fill(n=1):
            nonlocal filler
            for _ in range(n):
                if filler is None:
                    return
                try:
                    next(filler)
                except StopIteration:
                    filler = None
                    return

        def flush():
            while filler is not None:
                fill(1)

        gs0_of = {}

        def prefetch_x(ti):
            par = ti % 2
            r0 = T * ti
            sc.dma("sp", "xld", lambda e: e.dma_start(
                out=xs2[:, par, :, :], in_=x.rearrange("(b p) d -> p b d", p=128)[:, 4 * ti:4 * ti + 4, :]),
                writes=[c for b in range(4) for c in xsc(par, b)])
            if ti == 0:
                sc.op("dve", lambda e: e.memset(halo[:], 0.0), writes=["halo"])
            else:
                sc.dma("sp", "hld", lambda e: e.dma_start(out=halo[:], in_=x[r0 - 128:r0, :]),
                       writes=["halo"])
            gs0_of[ti] = gb_load(0)
            sc.op("act", lambda e: e.activation(xb[:, 0, :], halo[:], AF.Copy), reads=["halo"], writes=xb_cells(0))
            if ti == 0 and PREP:
                prep_wgu(0)
                prep_wd(0)
                prep_attn()
                prep_wgu(1)
                prep_wd(1)
            for b in range(4):
                if b < 2:
                    sc.op("act", lambda e, b=b: e.activation(xb[:, b + 1, :], xs2[:, par, b, :], AF.Copy),
                          reads=xsc(par, b), writes=xb_cells(b + 1))
                else:
                    sc.op("dve", lambda e, b=b: e.tensor_copy(xb[:, b + 1, :], xs2[:, par, b, :]),
                          reads=xsc(par, b), writes=xb_cells(b + 1))

        for i in range(NT):
            sc.epoch = i
            r0 = T * i
            par = i % 2
            if i == 0:
                prefetch_x(0)
            gs0 = gs0_of[i]
            for j in range(8):
                g = j // 2
                kk = j % 8
                for b in range(4):
                    cur = (8 + g) if (i == 0 and b == 0) else g
                    sc.op("pe", lambda e, j=j, b=b, cur=cur, kk=kk: e.matmul(
                        ps[:, kk, 128 * b:128 * (b + 1)], xb[:, b + 1, 128 * j:128 * (j + 1)], band[:, cur, :],
                        start=True, stop=False),
                        reads=xb_cells(b + 1) + ["band"], writes=pcell(kk), signal=False)
                    sc.op("pe", lambda e, j=j, b=b, g=g, kk=kk: e.matmul(
                        ps[:, kk, 128 * b:128 * (b + 1)], xb[:, b, 128 * j:128 * (j + 1)], band[:, 4 + g, :],
                        start=False, stop=True),
                        reads=xb_cells(b) + ["band"], writes=pcell(kk), signal=(b == 3))
                if j % 2 == 0:
                    sc.op("act", lambda e, j=j, kk=kk: e.activation(mo[:, j, :], bank(kk), AF.Copy),
                          reads=pcell(kk), writes=[("mo", j)])
                else:
                    sc.op("dve", lambda e, j=j, kk=kk: e.tensor_copy(mo[:, j, :], bank(kk)),
                          reads=pcell(kk), writes=[("mo", j)])
            for b in range(4):
                for g in range(4):
                    for k in range(2):
                        sc.op("pe", lambda e, b=b, g=g, k=k: e.matmul(
                            ps[:, 2 * b + g // 2, (g % 2) * 256:(g % 2) * 256 + 256],
                            mo[:, 2 * g + k, 128 * b:128 * (b + 1)], wp[:, 2 * g + k, :],
                            start=(k == 0), stop=(k == 1)),
                            reads=[("mo", 2 * g + k)] + wp_cells, writes=dpcell(b), signal=(g == 3 and k == 1))
                sc.op("dve", lambda e, b=b: e.scalar_tensor_tensor(
                    xs2[:, par, b, :], xs2[:, par, b, :], ALPHA, dbank(b), ALU.mult, ALU.add),
                    reads=xsc(par, b) + dpcell(b), writes=xsc(par, b))
            ln4(par, gs0, 0)
            for b in range(4):
                transpose_block(par, b, xTA, "xTA")
            ffn0(par)
            for b in range(4):
                transpose_block(par, b, xTA, "xTA")
            xT_cells = [("xTA", b) for b in range(4)]
            for op2 in range(8):
                slot, scell = ring_load(wqk_s[2 * op2:2 * op2 + 2].rearrange("o p k f -> p o k f"),
                                        lambda r: r.rearrange("p (a k f) -> p a k f", a=2, k=8), ["wqk_s"])
                for a in range(2):
                    oc = 2 * op2 + a
                    kk = oc % 8
                    for kc in range(8):
                        sc.op("pe", lambda e, a=a, kk=kk, kc=kc: e.matmul(
                            bank(kk), slot[:, a, kc, :], xTA[:, kc, :], start=(kc == 0), stop=(kc == 7)),
                            reads=scell + xT_cells, writes=pcell(kk), signal=(kc == 7))
                    dst = qkt[:, oc, :] if oc < 8 else kst[:, oc - 8, :]
                    dcell = [("qkt", oc)] if oc < 8 else [kst_cell[oc - 8]]
                    if oc % 2 == 0:
                        sc.op("act", lambda e, kk=kk: e.activation(dst, bank(kk), AF.Copy),
                              reads=pcell(kk), writes=dcell)
                    else:
                        sc.op("dve", lambda e, kk=kk: e.tensor_copy(dst, bank(kk)),
                              reads=pcell(kk), writes=dcell)
            for dh in range(2):
                for grp in range(2):
                    src = wv_s.rearrange("(kc p) d -> p kc d", p=128)[:, 4 * grp:4 * grp + 4, dh * 512:(dh + 1) * 512]
                    slot, scell = ring_load(src, lambda r: r.rearrange("p (a d) -> p a d", d=512), ["wv_s"])
                    for ki in range(4):
                        kc = 4 * grp + ki
                        for b in range(4):
                            kk = 4 * dh + b
                            sc.op("pe", lambda e, ki=ki, kc=kc, b=b, kk=kk: e.matmul(
                                bank(kk), xTA[:, kc, 128 * b:128 * (b + 1)], slot[:, ki, :],
                                start=(kc == 0), stop=(kc == 7)),
                                reads=scell + [("xTA", b)], writes=pcell(kk), signal=(kc == 7 or ki == 3))
                for b in range(4):
                    kk = 4 * dh + b
                    if b % 2 == 0:
                        sc.op("act", lambda e, b=b, kk=kk, dh=dh: e.activation(
                            vt[:, b, dh * 512:(dh + 1) * 512], bank(kk), AF.Copy),
                            reads=pcell(kk), writes=[vt_cell[b]])
                    else:
                        sc.op("dve", lambda e, b=b, kk=kk, dh=dh: e.tensor_copy(
                            vt[:, b, dh * 512:(dh + 1) * 512], bank(kk)),
                            reads=pcell(kk), writes=[vt_cell[b]])
            sc.dma("sp", "kw", lambda e: e.dma_start(
                out=kt_s.rearrange("h p s -> p h s")[:, :, r0:r0 + T], in_=kst),
                reads=[("P", 0), ("P", 1), ("P", 2), ("a", 0)], writes=["kt_s"])
            sc.dma("sp", "vw", lambda e: e.dma_start(
                out=v_s.rearrange("(b p) d -> p b d", p=128)[:, 4 * i:4 * i + 4, :], in_=vt[:, :, :]),
                reads=vt_cell, writes=["v_s"])

            nkb = 4 * i + 4
            nkeys = 128 * nkb
            steps = [(hp, k) for hp in range(8) for k in range(nkb)]
            nst = len(steps)

            def kv_load(hp):
                pb = hp % 2
                sc.dma("sp", f"kld{pb}", lambda e: e.dma_start(
                    out=kbuf[pb][:, 0:nkeys], in_=kt_s[hp][:, 0:nkeys]),
                    reads=["kt_s"], writes=kbuf_cells[pb])
                sc.dma("sp", f"vld{pb}", lambda e: e.dma_start(
                    out=vbuf[pb][:, 0:nkb, :],
                    in_=v_s.rearrange("(k p) d -> p k d", p=128)[:, 0:nkb, 128 * hp:128 * (hp + 1)]),
                    reads=["v_s"], writes=vbuf_cells[pb])

            def c0_of(m):
                hp, k = steps[m]
                kb = nkb - 1 - k
                return 128 * (kb - 4 * i) if kb >= 4 * i else 0

            def v3(ap2, c0):
                return ap2.rearrange("p (l n) -> p l n", l=2)[:, :, c0:512]

            def st_Z(m):
                hp, k = steps[m]
                kb = nkb - 1 - k
                diag = kb >= 4 * i
                jj = kb - 4 * i
                c0 = c0_of(m)
                for l in range(2):
                    sc.op("pe", lambda e, l=l: e.matmul(
                        ps[:, l, c0:512], kbuf[hp % 2][64 * l:64 * l + 64, 128 * kb:128 * (kb + 1)],
                        qkt[64 * l:64 * l + 64, hp, c0:512], start=True, stop=(not diag)),
                        reads=kbuf_cells[hp % 2] + [("qkt", hp)], writes=dpcell(0),
                        signal=(l == 1 and not diag))
                if diag:
                    for l in range(2):
                        sc.op("pe", lambda e, l=l: e.matmul(
                            ps[:, l, c0:512], tri[:, 2, :], mask[:, jj, c0:512], start=False, stop=True),
                            reads=["tri", "mask"], writes=dpcell(0), signal=(l == 1))

            def st_EP(m):
                c0 = c0_of(m)
                sc.op("act", lambda e: e.activation(v3(Eb[:, m % 3, :], c0), ps[:, 0:2, c0:512], AF.Exp, scale=0.125),
                      reads=dpcell(0), writes=[("E", m % 3)])
                sc.op("act", lambda e: e.activation(v3(Pb[:, m % 3, :], c0), v3(Eb[:, m % 3, :], c0), AF.Ln, bias=1.0),
                      reads=[("E", m % 3)], writes=[("P", m % 3)])

            def st_C(m):
                hp, k = steps[m]
                c0 = c0_of(m)
                for l in range(2):
                    if k == 0:
                        sc.op("pe", lambda e, l=l: e.matmul(
                            ps[:, 2 + l, :], tri[:, 3, :], mask[:, 0, :], start=True, stop=True,
                            skip_group_check=True),
                            reads=["tri", "mask"], writes=dpcell(1), signal=False)
                    else:
                        cp = c0_of(m - 1)
                        sc.op("pe", lambda e, l=l: e.matmul(
                            ps[:, 2 + l, cp:512], tri[:, 1, :], Pb[:, (m - 1) % 3, 512 * l + cp:512 * (l + 1)],
                            start=False, stop=False, skip_group_check=True),
                            reads=[("P", (m - 1) % 3), "tri"], writes=dpcell(1), signal=False)
                    sc.op("pe", lambda e, l=l: e.matmul(
                        ps[:, 2 + l, c0:512], tri[:, 0, :], Pb[:, m % 3, 512 * l + c0:512 * (l + 1)],
                        start=False, stop=True, skip_group_check=True),
                        reads=[("P", m % 3), "tri"], writes=dpcell(1), signal=(l == 1))

            def st_G(m):
                c0 = c0_of(m)
                sc.op("act", lambda e: e.activation(v3(Gb[:, m % 2, :], c0), ps[:, 2:4, c0:512], AF.Exp, scale=-1.0),
                      reads=dpcell(1), writes=[("G", m % 2)])

            def st_A(m):
                c0 = c0_of(m)
                sc.op("dve", lambda e: e.tensor_tensor(v3(ab[:, m % 2, :], c0), v3(Eb[:, m % 3, :], c0),
                                                       v3(Gb[:, m % 2, :], c0), ALU.mult),
                      reads=[("E", m % 3), ("G", m % 2)], writes=[("a", m % 2)])

            def st_O(m):
                hp, k = steps[m]
                kb = nkb - 1 - k
                c0 = c0_of(m)
                for l in range(2):
                    if k == 0:
                        sc.op("pe", lambda e, l=l: e.matmul(
                            ps[64 * l:64 * l + 64, 4, :], tri[:, 3, 0:64], mask[:, 0, :], start=True, stop=False,
                            skip_group_check=True),
                            reads=["tri", "mask"], writes=pcell(4), signal=False)
                    sc.op("pe", lambda e, l=l: e.matmul(
                        ps[64 * l:64 * l + 64, 4, c0:512], vbuf[hp % 2][:, kb, 64 * l:64 * l + 64],
                        ab[:, m % 2, 512 * l + c0:512 * (l + 1)], start=False, stop=(k == nkb - 1),
                        skip_group_check=True),
                        reads=vbuf_cells[hp % 2] + [("a", m % 2)], writes=pcell(4), signal=(l == 1))
                if k == nkb - 1:
                    sc.op("dve", lambda e: e.tensor_copy(mo[:, hp, :], bank(4)),
                          reads=pcell(4), writes=[("mo", hp)])

            frate = min(2.0, FR / nst)
            facc = 0.0
            kv_load(0)
            st_Z(0)
            for m in range(nst + 2):
                if m < nst:
                    hp, k = steps[m]
                    if k == 2 and hp + 1 < 8:
                        kv_load(hp + 1)
                    st_EP(m)
                if 0 <= m - 1 < nst:
                    st_C(m - 1)
                    st_G(m - 1)
                    st_A(m - 1)
                if m + 1 < nst:
                    st_Z(m + 1)
                    for _ in range(DUMMY):
                        st_Z(m + 1)
                if 0 <= m - 2 < nst:
                    st_O(m - 2)
                facc += frate
                while facc >= 1.0:
                    fill(1)
                    facc -= 1.0

            gs2 = gb_load(2)
            lns = []

            def advance_lns():
                for g in lns:
                    next(g, None)

            for b in range(4):
                for dh in range(2):
                    kk = (2 * b + dh) % 5
                    for kc in range(8):
                        sc.op("pe", lambda e, kc=kc, b=b, dh=dh, kk=kk: e.matmul(
                            bank(kk), mo[:, kc, 128 * b:128 * (b + 1)], wo_sb[:, kc, dh * 512:(dh + 1) * 512],
                            start=(kc == 0), stop=(kc == 7)),
                            reads=["wo_sb", ("mo", kc)], writes=pcell(kk), signal=(kc == 7))
                    sc.op("dve", lambda e, b=b, dh=dh, kk=kk: e.scalar_tensor_tensor(
                        xs2[:, par, b, dh * 512:(dh + 1) * 512], xs2[:, par, b, dh * 512:(dh + 1) * 512], ALPHA, bank(kk),
                        ALU.mult, ALU.add),
                        reads=xsc(par, b) + pcell(kk), writes=xsc(par, b))
                    advance_lns()
                lns.append(ln_gen(par, b, gs2, 0))
                fill(1)
            for _ in range(6):
                advance_lns()
                fill(1)
            flush()
            if i + 1 < NT:
                prefetch_x(i + 1)
            for b in range(4):
                transpose_block(par, b, xTB, "xTB")
            filler = ffn1_gen(i)
        flush()
        sc.wait_only("sp", out_evs)

        sems = {}
        for key in sorted(sc.keys, key=str):
            sems[key] = es.enter_context(nc.semaphore("s_" + "_".join(str(t) for t in key)))
        sc.replay(nc, sems)
    return nc


def make_consts():
    bf = ml_dtypes.bfloat16
    ident = np.eye(128, dtype=np.float32)
    j = np.arange(128)[:, None]
    s = np.arange(128)[None, :]
    tri = np.zeros((128, 4, 128), np.float32)
    tri[:, 0, :] = (j >= s)
    tri[:, 1, :] = (j < s)
    tri[:, 2, :] = -2048.0 * (j == s)
    mask = np.zeros((128, 4, 512), np.float32)
    t = np.arange(512)[None, :]
    for jj in range(4):
        mask[:, jj, :] = ((128 * jj + np.arange(128)[:, None]) >= t)
    band = np.zeros((128, 12, 128), np.float32)
    tp = np.arange(128)[:, None]
    tt = np.arange(128)[None, :]
    for g, w in enumerate(WINDOWS):
        cur = ((tp <= tt) & (tp > tt - w)).astype(np.float32) / w
        band[:, g, :] = cur - (tp == tt)
        band[:, 4 + g, :] = ((tp - 128) > (tt - w)).astype(np.float32) / w
        cnt = np.minimum(tt + 1, w).astype(np.float32)
        band[:, 8 + g, :] = ((tp <= tt) & (tp > tt - w)).astype(np.float32) / cnt - (tp == tt)
    return {"c_ident": ident, "c_tri": tri.astype(bf), "c_mask": mask.astype(bf), "c_band": band.astype(bf)}


def make_in_maps(x, ln_mix_g, ln_mix_b, ln_ffn_g, ln_ffn_b, pool_w, pool_scale, w_qkv, w_o, w_gate, w_up, w_down):
    f = lambda a: np.ascontiguousarray(np.asarray(a, dtype=np.float32))
    gb = np.stack([f(ln_mix_g)[0], f(ln_mix_b)[0], f(ln_ffn_g)[0], f(ln_ffn_b)[0],
                   f(ln_mix_g)[1], f(ln_mix_b)[1], f(ln_ffn_g)[1], f(ln_ffn_b)[1]], axis=0).reshape(-1)
    shared = {
        "gb": np.ascontiguousarray(gb), "pool_w": f(pool_w)[0], "pool_scale": f(pool_scale)[0],
        "w_qkv": f(w_qkv)[0], "w_o": f(w_o)[0], "w_gu": np.ascontiguousarray(np.stack([f(w_gate), f(w_up)], axis=1)),
        "w_down": f(w_down),
    }
    shared.update(make_consts())
    x = f(x)
    return [dict(shared, x=x[c]) for c in range(8)]


def kernel(x, ln_mix_g, ln_mix_b, ln_ffn_g, ln_ffn_b, pool_w, pool_scale, w_qkv, w_o, w_gate, w_up, w_down):
    in_maps = make_in_maps(x, ln_mix_g, ln_mix_b, ln_ffn_g, ln_ffn_b, pool_w, pool_scale,
                           w_qkv, w_o, w_gate, w_up, w_down)
    nc = build(NTILES)
    res = run_bass_kernel_spmd(nc, in_maps, core_ids=list(range(8)))
    return np.stack([np.asarray(r["out"]).astype(np.float32) for r in res.results], axis=0)
```
